# Optimizing a Trainium2 kernel written in Bass

```python
import jax, jax.numpy as jnp
from jax import lax
import numpy as np

D_MODEL = 1024
BATCH = 4
SEQ = 8192
DEPTH = 2

N_A = DEPTH // 2
N_B = DEPTH - N_A
MEM_LEN = 256
MEM_HEADS = 4
MEM_DH = D_MODEL // (2 * MEM_HEADS)
MEM_WIDTH = MEM_HEADS * MEM_DH
GLA_HEADS = 4
GLA_DV = D_MODEL // (2 * GLA_HEADS)
GLA_DK = GLA_DV // 2
GLA_QK = GLA_HEADS * GLA_DK
GLA_V = GLA_HEADS * GLA_DV
GLA_RANK = 16
GLA_TAU = 16.0
GLA_CHUNK = 64
NSA_HEADS = 8
NSA_GROUPS = 2
NSA_HPG = NSA_HEADS // NSA_GROUPS
NSA_DH = D_MODEL // (2 * NSA_HEADS)
NSA_WIDTH = NSA_HEADS * NSA_DH
KV_WIDTH = NSA_GROUPS * NSA_DH
CMP_BLOCK = 32
CMP_STRIDE = 16
CMP_HIDDEN = 256
SEL_BLOCK = 64
SEL_TOPN = 16
WINDOW = 512
Q_BLOCK = 128
FFN_DIM = ((8 * D_MODEL // 3 + 127) // 128) * 128
CONV_WIDTH = 3
EPS = 1e-6
A_PROJ = 2 * GLA_QK + 2 * GLA_V + GLA_RANK + MEM_WIDTH
B_PROJ = NSA_WIDTH + 3 * NSA_HEADS + MEM_WIDTH
SHARED_KV_PROJ = 6 * KV_WIDTH

kernel_name = 'yoco_gla_nsa_hybrid'

F32 = jnp.float32


def rmsnorm(x, g):
    xf = x.astype(F32)
    y = xf * lax.rsqrt(jnp.mean(xf * xf, axis=-1, keepdims=True) + EPS)
    return y.astype(x.dtype) * g


def split_cols(z, widths):
    idx = [int(i) for i in np.cumsum(widths)[:-1]]
    return jnp.split(z, idx, axis=-1)


def masked_softmax(s, mask):
    s = jnp.where(mask, s.astype(F32), -jnp.inf)
    m = jnp.max(s, axis=-1, keepdims=True)
    m = jnp.where(jnp.isfinite(m), m, 0.0)
    e = jnp.exp(s - m)
    return e / jnp.maximum(jnp.sum(e, axis=-1, keepdims=True), 1e-30)


def alibi_slopes(n):
    return jnp.exp2(-8.0 * jnp.arange(1, n + 1, dtype=F32) / n)


def mem_attend(mq, mem_k, mem_v):
    B, T, _ = mq.shape
    qh = mq.reshape(B, T, MEM_HEADS, MEM_DH) * (MEM_DH ** -0.5)
    s = jnp.einsum('bthd,bmhd->bhtm', qh, mem_k)
    p = jax.nn.softmax(s.astype(F32), axis=-1).astype(mem_v.dtype)
    return jnp.einsum('bhtm,bmhd->bthd', p, mem_v).reshape(B, T, MEM_WIDTH)


def gla_chunked(q, k, v, log_a):
    B, T, H, DK = q.shape
    DV = v.shape[-1]
    C = GLA_CHUNK
    NC = T // C

    def chunks(a):
        return a.astype(F32).reshape(B, NC, C, H, a.shape[-1]).transpose(1, 0, 3, 2, 4)

    causal = jnp.tril(jnp.ones((C, C), dtype=bool))[:, :, None]

    def step(S, inp):
        qc, kc, vc, gc = inp
        b = jnp.cumsum(gc, axis=2)
        b_last = b[:, :, -1:, :]
        o_inter = jnp.einsum('bhtd,bhde->bhte', qc * jnp.exp(b), S)
        decay = jnp.exp(jnp.where(causal, b[:, :, :, None, :] - b[:, :, None, :, :], -jnp.inf))
        attn = jnp.einsum('bhtd,bhsd,bhtsd->bhts', qc, kc, decay)
        o = o_inter + jnp.einsum('bhts,bhse->bhte', attn, vc)
        S = jnp.exp(b_last[:, :, 0, :])[..., None] * S + jnp.einsum('bhsd,bhse->bhde', kc * jnp.exp(b_last - b), vc)
        return S, o

    S0 = jnp.zeros((B, H, DK, DV), F32)
    _, o = lax.scan(step, S0, (chunks(q), chunks(k), chunks(v), chunks(log_a)))
    return o.transpose(1, 0, 3, 2, 4).reshape(B, T, H, DV)


def gla_layer_mix(h, mem_k, mem_v, w_in, w_alpha, b_alpha, g_head, w_out):
    B, T, _ = h.shape
    q, k, v, r, alr, mq = split_cols(h @ w_in, [GLA_QK, GLA_QK, GLA_V, GLA_V, GLA_RANK, MEM_WIDTH])
    q = q.reshape(B, T, GLA_HEADS, GLA_DK) * (GLA_DK ** -0.5)
    k = k.reshape(B, T, GLA_HEADS, GLA_DK)
    v = v.reshape(B, T, GLA_HEADS, GLA_DV)
    log_a = jax.nn.log_sigmoid((alr @ w_alpha + b_alpha).astype(F32)) / GLA_TAU
    log_a = log_a.reshape(B, T, GLA_HEADS, GLA_DK)
    o = gla_chunked(q, k, v, log_a).astype(h.dtype)
    o = rmsnorm(o, g_head).reshape(B, T, GLA_V) * jax.nn.silu(r)
    m = mem_attend(mq, mem_k, mem_v)
    return jnp.concatenate([o, m], axis=-1) @ w_out


def compress_blocks(a, pe, w1, w2):
    B, T, G, DH = a.shape
    n_sub = CMP_BLOCK // CMP_STRIDE
    sub = a.reshape(B, T // CMP_STRIDE, CMP_STRIDE, G, DH)
    ncmp = T // CMP_STRIDE - n_sub + 1
    blocks = jnp.concatenate([sub[:, i:i + ncmp] for i in range(n_sub)], axis=2)
    blocks = blocks + pe[:, None, :]
    flat = blocks.transpose(0, 1, 3, 2, 4).reshape(B, ncmp, G, CMP_BLOCK * DH)
    return jax.nn.gelu(flat @ w1) @ w2


def nsa_shared_kv(x, g_kv, w_kv, pe_k, pe_v, w_ck1, w_ck2, w_cv1, w_cv2):
    B, T, _ = x.shape
    kv = (rmsnorm(x, g_kv) @ w_kv).reshape(B, T, 6, NSA_GROUPS, NSA_DH)
    k_c, v_c, k_s, v_s, k_w, v_w = [kv[:, :, i] for i in range(6)]
    return (compress_blocks(k_c, pe_k, w_ck1, w_ck2), compress_blocks(v_c, pe_v, w_cv1, w_cv2),
            k_s, v_s, k_w, v_w)


def selection_overlap(ncmp, ns):
    cs = jnp.arange(ncmp) * CMP_STRIDE
    ss = jnp.arange(ns) * SEL_BLOCK
    ov = jnp.minimum(cs[:, None] + CMP_BLOCK, ss[None, :] + SEL_BLOCK) - jnp.maximum(cs[:, None], ss[None, :])
    return jnp.clip(ov, 0, None).astype(F32) / CMP_BLOCK


def nsa_attend(q, gates, k_cmp, v_cmp, k_slc, v_slc, k_win, v_win):
    B, T, H, DH = q.shape
    G, HPG = NSA_GROUPS, NSA_HPG
    nqb = T // Q_BLOCK
    ncmp = k_cmp.shape[1]
    ns = T // SEL_BLOCK
    n_sel = min(SEL_TOPN, ns)
    slope = alibi_slopes(H).reshape(G, HPG)[None, None, :, :, None]
    cmp_end = jnp.arange(ncmp) * CMP_STRIDE + CMP_BLOCK - 1
    overlap = selection_overlap(ncmp, ns)

    def to_blocks(a):
        return a.reshape(B, ns, SEL_BLOCK, G, DH).transpose(0, 3, 1, 2, 4)

    ks_b, vs_b = to_blocks(k_slc), to_blocks(v_slc)
    pad = ((0, 0), (WINDOW, 0), (0, 0), (0, 0))
    kw_pad, vw_pad = jnp.pad(k_win, pad), jnp.pad(v_win, pad)
    gather = jax.vmap(jax.vmap(lambda blocks, ix: blocks[ix]))
    sel_off = jnp.arange(SEL_BLOCK)
    win_off = jnp.arange(Q_BLOCK + WINDOW)
    blk = jnp.arange(ns)

    def one_block(inp):
        qi, gi, bi = inp
        t = bi * Q_BLOCK + jnp.arange(Q_BLOCK)
        dist_c = (t[:, None] - cmp_end[None, :])
        s = jnp.einsum('bqghd,bngd->bqghn', qi, k_cmp).astype(F32) - slope * dist_c.astype(F32)[None, :, None, None, :]
        p_c = masked_softmax(s, (dist_c >= 0)[None, :, None, None, :])
        o_c = jnp.einsum('bqghn,bngd->bqghd', p_c.astype(v_cmp.dtype), v_cmp)
        cur = t // SEL_BLOCK
        imp = jnp.einsum('bqghn,nj->bqgj', p_c, overlap)
        valid = (blk[None, :] <= cur[:, None])[None, :, None, :]
        forced = ((blk[None, :] == 0) | (blk[None, :] == cur[:, None]) | (blk[None, :] == cur[:, None] - 1))[None, :, None, :]
        imp = jnp.where(forced, jnp.inf, jnp.where(valid, imp, -jnp.inf))
        _, idx = lax.top_k(imp, n_sel)
        idx = idx.transpose(0, 2, 1, 3)
        kg = gather(ks_b, idx).reshape(B, G, Q_BLOCK, n_sel * SEL_BLOCK, DH)
        vg = gather(vs_b, idx).reshape(B, G, Q_BLOCK, n_sel * SEL_BLOCK, DH)
        pos = (idx[..., None] * SEL_BLOCK + sel_off).reshape(B, G, Q_BLOCK, n_sel * SEL_BLOCK)
        dist_s = (t[None, None, :, None] - pos).transpose(0, 2, 1, 3)[:, :, :, None, :]
        s = jnp.einsum('bqghd,bgqsd->bqghs', qi, kg).astype(F32) - slope * dist_s.astype(F32)
        p_s = masked_softmax(s, dist_s >= 0)
        o_s = jnp.einsum('bqghs,bgqsd->bqghd', p_s.astype(vg.dtype), vg)
        start = bi * Q_BLOCK
        kw = lax.dynamic_slice_in_dim(kw_pad, start, Q_BLOCK + WINDOW, axis=1)
        vw = lax.dynamic_slice_in_dim(vw_pad, start, Q_BLOCK + WINDOW, axis=1)
        kpos = start - WINDOW + win_off
        dist_w = t[:, None] - kpos[None, :]
        mask_w = ((dist_w >= 0) & (dist_w < WINDOW) & (kpos >= 0)[None, :])[None, :, None, None, :]
        s = jnp.einsum('bqghd,bkgd->bqghk', qi, kw).astype(F32) - slope * dist_w.astype(F32)[None, :, None, None, :]
        p_w = masked_softmax(s, mask_w)
        o_w = jnp.einsum('bqghk,bkgd->bqghd', p_w.astype(vw.dtype), vw)
        return gi[..., 0:1] * o_c + gi[..., 1:2] * o_s + gi[..., 2:3] * o_w

    qb = q.reshape(B, nqb, Q_BLOCK, G, HPG, DH).transpose(1, 0, 2, 3, 4, 5)
    gb = gates.reshape(B, nqb, Q_BLOCK, G, HPG, 3).transpose(1, 0, 2, 3, 4, 5).astype(q.dtype)
    out = lax.map(one_block, (qb, gb, jnp.arange(nqb)))
    return out.transpose(1, 0, 2, 3, 4, 5).reshape(B, T, H * DH)


def nsa_layer_mix(h, shared, mem_k, mem_v, w_in, w_out):
    B, T, _ = h.shape
    q, gl, mq = split_cols(h @ w_in, [NSA_WIDTH, 3 * NSA_HEADS, MEM_WIDTH])
    q = q.reshape(B, T, NSA_HEADS, NSA_DH) * (NSA_DH ** -0.5)
    gates = jax.nn.sigmoid(gl.reshape(B, T, NSA_HEADS, 3))
    o = nsa_attend(q, gates, *shared)
    m = mem_attend(mq, mem_k, mem_v)
    return jnp.concatenate([o, m], axis=-1) @ w_out


def conv_ffn(h, w_up, conv_w, conv_b, w_down):
    T = h.shape[1]
    a, b = jnp.split(h @ w_up, 2, axis=-1)
    a_pad = jnp.pad(a, ((0, 0), (CONV_WIDTH - 1, 0), (0, 0)))
    a = sum(a_pad[:, j:j + T] * conv_w[j] for j in range(CONV_WIDTH)) + conv_b
    return (jax.nn.silu(a) * b) @ w_down


def setup_inputs(seed: int = 0) -> dict:
    key = jax.random.key(seed)
    ks = jax.random.split(key, 26)

    def nrm(k, shape, scale):
        return jax.random.normal(k, shape, F32) * scale

    def gain(k, shape):
        return 1.0 + 0.02 * jax.random.normal(k, shape, F32)

    D = D_MODEL
    return {
        'x': nrm(ks[0], (BATCH, SEQ, D), 1.0),
        'mem': nrm(ks[1], (BATCH, MEM_LEN, D), 1.0),
        'g_mix': gain(ks[2], (DEPTH, D)),
        'g_ffn': gain(ks[3], (DEPTH, D)),
        'g_mem': gain(ks[4], (DEPTH, D)),
        'w_mem_kv': nrm(ks[5], (DEPTH, D, 2 * MEM_WIDTH), D ** -0.5),
        'w_up': nrm(ks[6], (DEPTH, D, 2 * FFN_DIM), D ** -0.5),
        'conv_w': nrm(ks[7], (DEPTH, CONV_WIDTH, FFN_DIM), CONV_WIDTH ** -0.5),
        'conv_b': nrm(ks[8], (DEPTH, FFN_DIM), 0.01),
        'w_down': nrm(ks[9], (DEPTH, FFN_DIM, D), FFN_DIM ** -0.5),
        'a_w_in': nrm(ks[10], (N_A, D, A_PROJ), D ** -0.5),
        'a_w_alpha': nrm(ks[11], (N_A, GLA_RANK, GLA_QK), GLA_RANK ** -0.5),
        'a_b_alpha': nrm(ks[12], (N_A, GLA_QK), 0.01),
        'a_g_head': gain(ks[13], (N_A, GLA_DV)),
        'a_w_out': nrm(ks[14], (N_A, GLA_V + MEM_WIDTH, D), (GLA_V + MEM_WIDTH) ** -0.5),
        'g_kv': gain(ks[15], (D,)),
        'w_kv': nrm(ks[16], (D, SHARED_KV_PROJ), D ** -0.5),
        'pe_k': nrm(ks[17], (CMP_BLOCK, NSA_DH), 0.1),
        'pe_v': nrm(ks[18], (CMP_BLOCK, NSA_DH), 0.1),
        'w_ck1': nrm(ks[19], (CMP_BLOCK * NSA_DH, CMP_HIDDEN), (CMP_BLOCK * NSA_DH) ** -0.5),
        'w_ck2': nrm(ks[20], (CMP_HIDDEN, NSA_DH), CMP_HIDDEN ** -0.5),
        'w_cv1': nrm(ks[21], (CMP_BLOCK * NSA_DH, CMP_HIDDEN), (CMP_BLOCK * NSA_DH) ** -0.5),
        'w_cv2': nrm(ks[22], (CMP_HIDDEN, NSA_DH), CMP_HIDDEN ** -0.5),
        'b_w_in': nrm(ks[23], (N_B, D, B_PROJ), D ** -0.5),
        'b_w_out': nrm(ks[24], (N_B, NSA_WIDTH + MEM_WIDTH, D), (NSA_WIDTH + MEM_WIDTH) ** -0.5),
        'g_final': gain(ks[25], (D,)),
    }


def reference(x, mem, g_mix, g_ffn, g_mem, w_mem_kv, w_up, conv_w, conv_b, w_down,
              a_w_in, a_w_alpha, a_b_alpha, a_g_head, a_w_out,
              g_kv, w_kv, pe_k, pe_v, w_ck1, w_ck2, w_cv1, w_cv2,
              b_w_in, b_w_out, g_final):
    B, M, _ = mem.shape
    shared = None
    for l in range(DEPTH):
        if l == N_A:
            shared = nsa_shared_kv(x, g_kv, w_kv, pe_k, pe_v, w_ck1, w_ck2, w_cv1, w_cv2)
        mkv = (rmsnorm(mem, g_mem[l]) @ w_mem_kv[l]).reshape(B, M, 2, MEM_HEADS, MEM_DH)
        mem_k, mem_v = mkv[:, :, 0], mkv[:, :, 1]
        h = rmsnorm(x, g_mix[l])
        if l < N_A:
            x = x + gla_layer_mix(h, mem_k, mem_v, a_w_in[l], a_w_alpha[l], a_b_alpha[l], a_g_head[l], a_w_out[l])
        else:
            j = l - N_A
            x = x + nsa_layer_mix(h, shared, mem_k, mem_v, b_w_in[j], b_w_out[j])
        x = x + conv_ffn(rmsnorm(x, g_ffn[l]), w_up[l], conv_w[l], conv_b[l], w_down[l])
    return rmsnorm(x, g_final)
```

```python
import numpy as np
from contextlib import ExitStack
import ml_dtypes
import concourse.bass as bass
import concourse.mybir as mybir
from concourse.bass_utils import run_bass_kernel_spmd

F32 = mybir.dt.float32
BF16 = mybir.dt.bfloat16
ALU = mybir.AluOpType
AF = mybir.ActivationFunctionType
AX = mybir.AxisListType

D = 1024
MEM_LEN = 256
FFN = 2816
NFC = FFN // 128
EPS = 1e-6
A_PROJ = 2064
B_PROJ = 1048


class _Op:
    __slots__ = ("eng", "fn", "deps", "signal", "dma", "semid", "target")


class Prog:
    NDMASEM = 24
    ARENA_WORDS = 51200
    ENGS = ("pe", "act", "dve", "pool", "sp")

    def __init__(self):
        self.nc = bass.Bass("TRN2", target_bir_lowering=False)
        self.es = ExitStack()
        self.ops = []
        self.lastw = {}
        self.rd_eng = {}
        self.rd_dma = {}
        self.dma_rr = 0
        self.dma_last = [None] * self.NDMASEM
        self.dma_cnt = [0] * self.NDMASEM
        self.arena = None
        self.amax = 0

    def dram(self, name, shape, dt, kind="Internal"):
        return self.nc.dram_tensor(name, list(shape), dt, kind=kind).ap()

    def sb(self, name, shape, dt):
        if self.arena is None:
            self.arena = self.es.enter_context(self.nc.sbuf_tensor("arena", [128, self.ARENA_WORDS], F32))
            self.aoff = 0
        n = 1
        for d in shape[1:]:
            n *= d
        words = (n * (4 if dt == F32 else 2) + 31) // 32 * 8
        assert self.aoff + words <= self.ARENA_WORDS, ("SBUF arena overflow", name, self.aoff, words)
        v = self.arena[0:shape[0], self.aoff:self.aoff + words]
        if dt != F32:
            v = v.bitcast(dt)
        v = v[:, 0:n]
        if len(shape) == 3:
            v = v.rearrange("p (a b) -> p a b", a=shape[1])
        elif len(shape) == 4:
            v = v.rearrange("p (a b c) -> p a b c", a=shape[1], b=shape[2])
        self.aoff += words
        self.amax = max(self.amax, self.aoff)
        return v

    def barrier(self):
        deps = set(x for x in self.dma_last if x is not None)
        last = {}
        for i, o in enumerate(self.ops):
            if not o.dma:
                last[o.eng] = i
        deps.update(last.values())
        for eng in self.ENGS:
            i = self.op(eng, lambda e: e.nop())
            self.ops[i].deps = sorted(deps)

    def ps(self, name, shape, dt=F32):
        return self.es.enter_context(self.nc.psum_tensor(name, list(shape), dt))

    def op(self, eng, fn, reads=(), writes=(), dma=False):
        i = len(self.ops)
        deps = set()
        for k in reads:
            w = self.lastw.get(k)
            if w is not None:
                deps.add(w)
        for k in writes:
            w = self.lastw.get(k)
            if w is not None:
                deps.add(w)
            deps.update(self.rd_eng.get(k, {}).values())
            deps.update(self.rd_dma.get(k, ()))
        o = _Op()
        o.eng, o.fn, o.signal, o.dma = eng, fn, False, dma
        o.semid, o.target = None, None
        if dma:
            s = self.dma_rr % self.NDMASEM
            self.dma_rr += 1
            if self.dma_last[s] is not None:
                deps.add(self.dma_last[s])
            self.dma_cnt[s] += 1
            o.semid, o.target = s, 16 * self.dma_cnt[s]
            self.dma_last[s] = i
        for k in reads:
            if dma:
                self.rd_dma.setdefault(k, []).append(i)
            else:
                self.rd_eng.setdefault(k, {})[eng] = i
        for k in writes:
            self.lastw[k] = i
            self.rd_eng[k] = {}
            self.rd_dma[k] = []
        deps.discard(i)
        o.deps = sorted(d for d in deps if not (eng == "pe" and self.ops[d].eng == "pe" and not self.ops[d].dma))
        self.ops.append(o)
        return i

    def dma(self, eng, out, in_, reads=(), writes=(), **kw):
        return self.op(eng, lambda e: e.dma_start(out=out, in_=in_, **kw), reads, writes, dma=True)

    def finalize(self):
        nc = self.nc
        ops = self.ops
        for o in ops:
            for d in o.deps:
                ops[d].signal = True
        cnt = {e: 0 for e in self.ENGS}
        for o in ops:
            if not o.dma and o.signal:
                cnt[o.eng] += 1
                o.target = cnt[o.eng]
        esem = {e: self.es.enter_context(nc.semaphore("s_" + e)) for e in self.ENGS}
        dsem = [self.es.enter_context(nc.semaphore("d%d" % i)) for i in range(self.NDMASEM)]
        self.nwaits = 0

        def body(e, ename):
            waited = {}
            for o in ops:
                if o.eng != ename:
                    continue
                for d in o.deps:
                    pr = ops[d]
                    key = ("d", pr.semid) if pr.dma else ("e", pr.eng)
                    if waited.get(key, 0) >= pr.target:
                        continue
                    sem = dsem[pr.semid] if pr.dma else esem[pr.eng]
                    e.wait_ge(sem, pr.target)
                    self.nwaits += 1
                    waited[key] = pr.target
                ins = o.fn(e)
                if o.dma:
                    ins.then_inc(dsem[o.semid], 16)
                elif o.signal:
                    ins.then_inc(esem[o.eng], 1)

        with nc.Block() as block:
            @block.tensor
            def _(e):
                body(e, "pe")

            @block.scalar
            def _(e):
                body(e, "act")

            @block.vector
            def _(e):
                body(e, "dve")

            @block.gpsimd
            def _(e):
                body(e, "pool")

            @block.sync
            def _(e):
                body(e, "sp")
        self.es.close()
        return nc


def host_consts():
    c = {}
    c["ident"] = np.eye(128, dtype=np.float32)
    s = np.arange(128)[:, None]
    t = np.arange(128)[None, :]
    c["uinc"] = np.where(s <= t, -1.0 / 16.0, 0.0).astype(np.float32)
    c["urev"] = np.where(s > t, -1.0 / 16.0, 0.0).astype(np.float32)
    c["mask4"] = np.tile((s <= t).astype(np.float32), (1, 4))
    c["onesd"] = np.full((128, 128), 1.0 / 1024.0, dtype=np.float32)
    c["onesv"] = np.full((128, 128), 1.0 / 128.0, dtype=np.float32)
    c["onesb"] = np.ones((128, 128), dtype=ml_dtypes.bfloat16)
    return c


CONST_DT = {"ident": F32, "uinc": F32, "urev": F32, "mask4": F32, "onesd": F32, "onesv": F32, "onesb": BF16}


def build(T, dbg=False):
    TT = 512
    NT = T // TT
    p = Prog()

    def din(name, shape, dt=F32):
        return p.dram(name, shape, dt, kind="ExternalInput")

    def dtmp(name, shape, dt, out=False):
        return p.dram(name, shape, dt, kind="ExternalOutput" if (out and dbg) else "Internal")

    x_d = din("x", [T, D])
    mem_d = din("mem", [MEM_LEN, D])
    gv_d = din("gvec", [128, 8, 8])
    convw_d = din("convw", [128, 2, NFC, 3])
    convb_d = din("convb", [128, 2, NFC])
    ghead_d = din("ghead", [128, 1])
    walpha_d = din("walpha", [32, 256])
    W = {
        "a_w_in": din("a_w_in", [D, A_PROJ]), "a_w_out": din("a_w_out", [D, D]),
        "w_mem_kv0": din("w_mem_kv0", [D, D]), "w_mem_kv1": din("w_mem_kv1", [D, D]),
        "w_up0": din("w_up0", [D, 2 * FFN]), "w_up1": din("w_up1", [D, 2 * FFN]),
        "w_down0": din("w_down0", [FFN, D]), "w_down1": din("w_down1", [FFN, D]),
        "w_kv": din("w_kv", [D, 768]), "b_w_in": din("b_w_in", [D, B_PROJ]), "b_w_out": din("b_w_out", [D, D]),
        "w_ck1": din("w_ck1", [2048, 256]), "w_cv1": din("w_cv1", [2048, 256]),
        "w_ck2": din("w_ck2", [256, 64]), "w_cv2": din("w_cv2", [256, 64]),
        "wgp": din("wgp", [D, 24]),
    }
    pe_d = din("pe_kv", [128, 2, 2048])
    NCMP = T // 16 - 1
    NCT = (NCMP + 127) // 128
    NCP = NCT * 128
    NKB = T // 128
    NQB = T // 128
    ncst = nsa_consts(T)
    ncst_d = {k: din("n_" + k, list(v.shape), BF16 if v.dtype != np.float32 else F32) for k, v in ncst.items()}
    cst_d = {k: din("c_" + k, list(v.shape), CONST_DT[k]) for k, v in host_consts().items()}

    WB = {k: p.dram("wb_" + k, list(v.shape), BF16) for k, v in W.items()}
    xT_d = dtmp("xT", [D, T], F32, out=True)
    kvtm_d = dtmp("kvtm", [T, 768], F32, out=True)
    kaT_d = dtmp("kaT", [2, 2, 64, T], BF16)
    kcT_d = dtmp("kcT", [2, 64, NCP], BF16, out=True)
    vc_d = dtmp("vc", [2, NCP, 64], BF16, out=True)
    qT_d = dtmp("qT", [8, 64, T], BF16)
    gatesT_d = dtmp("gatesT", [24, T], F32)
    mT_d = dtmp("mT", [512, T], BF16, out=True)
    oT_d = dtmp("oT", [512, T], BF16, out=True)
    out_d = p.dram("out", [T, D], F32, kind="ExternalOutput")

    cst = {k: p.sb("k_" + k, list(v.shape), CONST_DT[k]) for k, v in host_consts().items()}
    gv = p.sb("gv", [128, 8, 8], F32)
    convw = p.sb("convw", [128, 2, NFC, 3], F32)
    convb = p.sb("convb", [128, 2, NFC], F32)
    ghead = p.sb("ghead", [128, 1], F32)
    walpha = p.sb("walpha", [32, 256], F32)
    NWB = 4
    wbuf = [p.sb("wbuf%d" % i, [128, 8, 512], BF16) for i in range(NWB)]
    walr = p.sb("walr", [128, 8, 16], BF16)
    memkT = [p.sb("memkT%d" % l, [128, 4, 256], BF16) for l in range(2)]
    memv = [p.sb("memv%d" % l, [128, 2, 512], BF16) for l in range(2)]
    wgl = p.sb("wgl", [128, 8, 24], BF16)
    markA = p.aoff
    xin = p.sb("xin", [128, 2, D], F32)
    xT = p.sb("xT", [128, 8, TT], F32)
    hT = p.sb("hT", [128, 8, TT], BF16)
    sqb = [p.sb("sq%d" % i, [128, TT], F32) for i in range(2)]
    rstd = p.sb("rstd", [128, TT], F32)
    alrT = p.sb("alrT", [32, TT], F32)
    sp_tm = p.sb("sp_tm", [128, 4, 256], F32)
    Ebl = p.sb("Ebl", [128, 4, 256], F32)
    Eb = p.sb("Eb", [64, 4, TT], F32)
    Enb = p.sb("Enb", [64, 4, TT], F32)
    qtT = p.sb("qtT", [64, 4, TT], BF16)
    ktT = p.sb("ktT", [64, 4, TT], BF16)
    kh_tm = p.sb("kh_tm", [128, 4, 256], BF16)
    v_tm = p.sb("v_tm", [128, 4, 512], BF16)
    sr = p.sb("sr", [128, 4, TT], BF16)
    mqT = p.sb("mqT", [128, 4, TT], BF16)
    catT = p.sb("catT", [128, 8, TT], BF16)
    S = p.sb("S", [64, 4, 128], F32)
    Sbf = p.sb("Sbf", [64, 4, 128], BF16)
    A_sb = p.sb("A_sb", [128, 512], BF16)
    o_sb = p.sb("o_sb", [128, 512], F32)
    osq = p.sb("osq", [128, 512], F32)
    orst = rstd
    pT = [p.sb("pT%d" % i, [128, TT], BF16) for i in range(2)]
    rz = rstd
    abuf = [p.sb("abuf%d" % i, [128, TT + 2], F32) for i in range(2)]
    acc = [p.sb("acc%d" % i, [128, TT], F32) for i in range(2)]
    halo = p.sb("halo", [128, NFC, 2], F32)
    hF = p.sb("hF", [128, NFC, TT], BF16)
    kvst = p.sb("kvst", [128, 768], F32)
    kaT_sb = p.sb("kaT_sb", [64, 4, TT], BF16)
    gat_sb = p.sb("gat_sb", [32, TT], F32)
    print("arena words", p.amax)
    psum = [p.ps("ps%d" % i, [128, 512], F32) for i in range(8)]

    st = {"ps": 0, "wb": 0, "sq": 0, "psmod": 8, "pt": 0}

    def newps():
        i = st["ps"] % st["psmod"]
        st["ps"] = (i + 1) % st["psmod"]
        return psum[i], ("ps", i)

    def mm(out, lhsT, rhs, start, stop, r, w):
        p.op("pe", lambda e: e.matmul(out, lhsT=lhsT, rhs=rhs, start=start, stop=stop), r, w)

    def tr(out, in_, r, w, n=128):
        idn = cst["ident"]
        p.op("pe", lambda e: e.transpose(out, in_, idn[0:n, 0:n]), list(r) + ["c_ident"], w)

    def act(out, in_, func, r, w, **kw):
        p.op("act", lambda e: e.activation(out=out, in_=in_, func=func, **kw), r, w)

    def cp(eng, out, in_, r, w):
        if eng == "act":
            p.op("act", lambda e: e.copy(out=out, in_=in_), r, w)
        else:
            p.op(eng, lambda e: e.tensor_copy(out=out, in_=in_), r, w)

    def tt(eng, out, in0, in1, op, r, w):
        p.op(eng, lambda e: e.tensor_tensor(out=out, in0=in0, in1=in1, op=op), r, w)

    def ts(eng, out, in0, s1, s2, op0, op1, r, w):
        if s2 is None:
            p.op(eng, lambda e: e.tensor_scalar(out=out, in0=in0, scalar1=s1, scalar2=None, op0=op0), r, w)
        else:
            p.op(eng, lambda e: e.tensor_scalar(out=out, in0=in0, scalar1=s1, scalar2=s2, op0=op0, op1=op1), r, w)

    def stt(eng, out, in0, scalar, in1, op0, op1, r, w):
        p.op(eng, lambda e: e.scalar_tensor_tensor(out=out, in0=in0, scalar=scalar, in1=in1, op0=op0, op1=op1), r, w)

    def rsqrt_eps(out, in_, r, wkey):
        act(out, in_, AF.Sqrt, r, [wkey], bias=EPS, scale=1.0)
        p.op("dve", lambda e: e.reciprocal(out=out, in_=out), [wkey], [wkey])

    def memset(eng, ap, val, w):
        p.op(eng, lambda e: e.memset(ap, val), (), w)

    for k in cst:
        p.dma("sp", cst[k][:], cst_d[k][:, :], writes=["c_" + k])
    p.dma("sp", gv[:], gv_d[:, :, :], writes=["gv"])
    p.dma("sp", convw[:], convw_d[:, :, :, :], writes=["convw"])
    p.dma("sp", convb[:], convb_d[:, :, :], writes=["convb"])
    p.dma("sp", ghead[:], ghead_d[:, :], writes=["ghead"])
    p.dma("sp", walpha[:], walpha_d[:, :], writes=["walpha"])
    for name, wd in W.items():
        rows = wd.shape[0]
        for k in range(rows // 128):
            p.dma("pool", WB[name][k * 128:(k + 1) * 128, :], wd[k * 128:(k + 1) * 128, :],
                  writes=[("wb", name, k)])

    def wload(name, c0, ncols, krows=D, r0=0):
        nk = krows // 128
        i = st["wb"]
        st["wb"] = (i + 1) % NWB
        src = WB[name][r0:r0 + krows, c0:c0 + ncols].rearrange("(k p) f -> p k f", p=128)
        assert nk * ncols <= 4096
        view = wbuf[i][:, :, :].rearrange("p k f -> p (k f)")[:, 0:nk * ncols].rearrange("p (k f) -> p k f", f=ncols)
        p.dma("sp", view, src,
              reads=[("wb", name, r0 // 128 + k) for k in range(nk)], writes=[("wbuf", i)])
        return view, ("wbuf", i)

    def rms_fm(src, srckeys, gi, N, dst, dstkey, nk=8):
        ps, pk = newps()
        for k in range(nk):
            j = st["sq"]
            st["sq"] = 1 - j
            act(sqb[j][:, 0:N], src[:, k, 0:N], AF.Square, [srckeys[k]], [("sq", j)])
            mm(ps[:, 0:N], cst["onesd"][:], sqb[j][:, 0:N], k == 0, k == nk - 1, [("sq", j), "c_onesd"], [pk])
        rsqrt_eps(rstd[:, 0:N], ps[:, 0:N], [pk], "rstd")
        for k in range(nk):
            stt("dve", dst[:, k, 0:N], src[:, k, 0:N], gv[:, gi, k:k + 1], rstd[:, 0:N], ALU.mult, ALU.mult,
                [srckeys[k], "rstd", "gv"], [(dstkey, k)])

    for l in range(2):
        for mc in range(2):
            p.dma("sp", xin[:, mc, :], mem_d[mc * 128:(mc + 1) * 128, :], writes=[("xin", mc)])
        for k in range(8):
            ps, pk = newps()
            for mc in range(2):
                tr(ps[:, mc * 128:(mc + 1) * 128], xin[:, mc, k * 128:(k + 1) * 128], [("xin", mc)], [pk])
            cp("act", xT[:, k, 0:256], ps[:, 0:256], [pk], [("xT", k)])
        rms_fm(xT, [("xT", k) for k in range(8)], 4 + l, 256, hT, "hT")
        hk = [("hT", k) for k in range(8)]
        wname = "w_mem_kv%d" % l
        wt, wk = wload(wname, 0, 512)
        for h in range(4):
            ps, pk = newps()
            for k in range(8):
                mm(ps[:, 0:256], wt[:, k, h * 128:(h + 1) * 128], hT[:, k, 0:256], k == 0, k == 7, [wk, hk[k]], [pk])
            cp("act", memkT[l][:, h, :], ps[:, 0:256], [pk], [("memkT", l)])
        wt, wk = wload(wname, 512, 512)
        for mc in range(2):
            ps, pk = newps()
            for k in range(8):
                mm(ps[:, :], hT[:, k, mc * 128:(mc + 1) * 128], wt[:, k, :], k == 0, k == 7, [wk, hk[k]], [pk])
            cp("act", memv[l][:, mc, :], ps[:, :], [pk], [("memv", l)])

    xk = [("xT", k) for k in range(8)]
    hk = [("hT", k) for k in range(8)]
    ck = [("catT", k) for k in range(8)]

    def mem_attend(l):
        for h in range(4):
            for mc in range(2):
                ps, pk = newps()
                mm(ps[:, :], memkT[l][:, h, mc * 128:(mc + 1) * 128], mqT[:, h, :], True, True,
                   [("memkT", l), ("mqT", h)], [pk])
                act(pT[mc][:, :], ps[:, :], AF.Exp, [pk], [("pT", mc)])
            pso, pko = newps()
            psz, pkz = newps()
            for mc in range(2):
                mm(pso[:, :], memv[l][:, mc, h * 128:(h + 1) * 128], pT[mc][:, :], mc == 0, mc == 1,
                   [("memv", l), ("pT", mc)], [pko])
            for mc in range(2):
                mm(psz[:, :], cst["onesb"][:], pT[mc][:, :], mc == 0, mc == 1, ["c_onesb", ("pT", mc)], [pkz])
            p.op("dve", lambda e, psz=psz: e.reciprocal(out=rz[:, :], in_=psz[:, :]), [pkz], ["rstd"])
            tt("dve", catT[:, 4 + h, :], pso[:, :], rz[:, :], ALU.mult, [pko, "rstd"], [("catT", 4 + h)])

    def outproj(wname):
        for half in range(2):
            wt, wk = wload(wname, half * 512, 512)
            for dcl in range(4):
                dc = half * 4 + dcl
                ps, pk = newps()
                for k in range(8):
                    mm(ps[:, :], wt[:, k, dcl * 128:(dcl + 1) * 128], catT[:, k, :], k == 0, k == 7, [wk, ck[k]], [pk])
                tt("dve", xT[:, dc, :], xT[:, dc, :], ps[:, :], ALU.add, [pk, xk[dc]], [xk[dc]])

    def ffn(l, first_tile):
        rms_fm(xT, xk, 2 + l, TT, hT, "hT")
        wup = "w_up%d" % l
        for g0 in range(0, NFC, 4):
            ng = min(4, NFC - g0)
            wa, wak = wload(wup, g0 * 128, ng * 128)
            wb_, wbk = wload(wup, FFN + g0 * 128, ng * 128)
            for j in range(ng):
                fc = g0 + j
                psa, pka = newps()
                for k in range(8):
                    mm(psa[:, :], wa[:, k, j * 128:(j + 1) * 128], hT[:, k, :], k == 0, k == 7, [wak, hk[k]], [pka])
                psb, pkb = newps()
                for k in range(8):
                    mm(psb[:, :], wb_[:, k, j * 128:(j + 1) * 128], hT[:, k, :], k == 0, k == 7, [wbk, hk[k]], [pkb])
                i = fc % 2
                ab, ac = abuf[i], acc[i]
                if first_tile:
                    memset("pool", ab[:, 0:2], 0.0, [("abuf", i)])
                else:
                    cp("pool", ab[:, 0:2], halo[:, fc, :], [("halo", fc)], [("abuf", i)])
                cp("act", ab[:, 2:TT + 2], psa[:, :], [pka], [("abuf", i)])
                cp("pool", halo[:, fc, :], ab[:, TT:TT + 2], [("abuf", i)], [("halo", fc)])
                ts("dve", ac[:, :], ab[:, 2:TT + 2], convw[:, l, fc, 2:3], convb[:, l, fc:fc + 1], ALU.mult, ALU.add,
                   [("abuf", i), "convw", "convb"], [("acc", i)])
                stt("dve", ac[:, :], ab[:, 1:TT + 1], convw[:, l, fc, 1:2], ac[:, :], ALU.mult, ALU.add,
                    [("abuf", i), ("acc", i), "convw"], [("acc", i)])
                stt("dve", ac[:, :], ab[:, 0:TT], convw[:, l, fc, 0:1], ac[:, :], ALU.mult, ALU.add,
                    [("abuf", i), ("acc", i), "convw"], [("acc", i)])
                act(ac[:, :], ac[:, :], AF.Silu, [("acc", i)], [("acc", i)])
                tt("dve", hF[:, fc, :], ac[:, :], psb[:, :], ALU.mult, [("acc", i), pkb], [("hF", fc)])
        wdn = "w_down%d" % l
        for half in range(2):
            pss = [newps() for _ in range(4)]
            for g0 in range(0, NFC, 4):
                ng = min(4, NFC - g0)
                wt, wk = wload(wdn, half * 512, 512, krows=ng * 128, r0=g0 * 128)
                for dcl in range(4):
                    for j in range(ng):
                        fc = g0 + j
                        mm(pss[dcl][0][:, :], wt[:, j, dcl * 128:(dcl + 1) * 128], hF[:, fc, :], fc == 0, fc == NFC - 1,
                           [wk, ("hF", fc)], [pss[dcl][1]])
            for dcl in range(4):
                dc = half * 4 + dcl
                tt("dve", xT[:, dc, :], xT[:, dc, :], pss[dcl][0][:, :], ALU.add, [pss[dcl][1], xk[dc]], [xk[dc]])

    memset("dve", alrT[:, :], 1.0, ["alrT"])
    memset("dve", S[:, :, :], 0.0, ["S"])
    memset("dve", Sbf[:, :, :], 0.0, ["Sbf"])
    for tix in range(NT):
        t0 = tix * TT
        for tb in range(4):
            xi = tb % 2
            p.dma("sp", xin[:, xi, :], x_d[t0 + tb * 128:t0 + (tb + 1) * 128, :], writes=[("xin", xi)])
            for half in range(2):
                ps, pk = newps()
                for kk in range(4):
                    k = half * 4 + kk
                    tr(ps[:, kk * 128:(kk + 1) * 128], xin[:, xi, k * 128:(k + 1) * 128], [("xin", xi)], [pk])
                cp("act" if half else "dve", xT[:, half * 4:half * 4 + 4, tb * 128:(tb + 1) * 128],
                   ps[:, :].rearrange("p (a b) -> p a b", a=4), [pk], [xk[half * 4 + kk] for kk in range(4)])
        rms_fm(xT, xk, 0, TT, hT, "hT")
        p.dma("sp", walr[:, :, :], WB["a_w_in"][:, 1536:1552].rearrange("(k p) f -> p k f", p=128),
              reads=[("wb", "a_w_in", k) for k in range(8)], writes=["walr"])
        ps, pk = newps()
        for k in range(8):
            mm(ps[0:16, :], walr[:, k, :], hT[:, k, :], k == 0, k == 7, ["walr", hk[k]], [pk])
        cp("act", alrT[0:16, :], ps[0:16, :], [pk], ["alrT"])
        for tb2 in range(2):
            ps, pk = newps()
            for j in range(2):
                tb = tb2 * 2 + j
                mm(ps[:, j * 256:(j + 1) * 256], alrT[:, tb * 128:(tb + 1) * 128], walpha[:, :], True, True,
                   ["alrT", "walpha"], [pk])
            act(sp_tm[:, tb2 * 2:tb2 * 2 + 2, :], ps[:, :].rearrange("p (a b) -> p a b", a=2), AF.Exp, [pk],
                [("sp_tm", tb2)], scale=-1.0)
            act(sp_tm[:, tb2 * 2:tb2 * 2 + 2, :], sp_tm[:, tb2 * 2:tb2 * 2 + 2, :], AF.Ln, [("sp_tm", tb2)],
                [("sp_tm", tb2)], bias=1.0)
        for tb2 in range(2):
            ps, pk = newps()
            for j in range(2):
                tb = tb2 * 2 + j
                mm(ps[:, j * 256:(j + 1) * 256], cst["urev"][:], sp_tm[:, tb, :], True, True,
                   ["c_urev", ("sp_tm", tb2)], [pk])
            act(Ebl[:, tb2 * 2:tb2 * 2 + 2, :], ps[:, :].rearrange("p (a b) -> p a b", a=2), AF.Exp, [pk],
                [("Ebl", tb2)])
        for h in range(4):
            ps, pk = newps()
            for tb in range(4):
                mm(ps[0:64, tb * 128:(tb + 1) * 128], sp_tm[:, tb, h * 64:(h + 1) * 64], cst["uinc"][:], True, True,
                   ["c_uinc", ("sp_tm", tb // 2)], [pk])
            act(Eb[:, h, :], ps[0:64, :], AF.Exp, [pk], [("Eb", h)])
            act(Enb[:, h, :], ps[0:64, :], AF.Exp, [pk], [("Enb", h)], scale=-1.0)
        wt, wk = wload("a_w_in", 0, 512)
        for h in range(4):
            ps, pk = newps()
            for k in range(8):
                mm(ps[0:64, :], wt[:, k, h * 64:(h + 1) * 64], hT[:, k, :], k == 0, k == 7, [wk, hk[k]], [pk])
            stt("dve", qtT[:, h, :], ps[0:64, :], 0.125, Eb[:, h, :], ALU.mult, ALU.mult, [pk, ("Eb", h)], [("qtT", h)])
            ps, pk = newps()
            for k in range(8):
                mm(ps[0:64, :], wt[:, k, 256 + h * 64:256 + (h + 1) * 64], hT[:, k, :], k == 0, k == 7, [wk, hk[k]], [pk])
            tt("dve", ktT[:, h, :], ps[0:64, :], Enb[:, h, :], ALU.mult, [pk, ("Enb", h)], [("ktT", h)])
        for tb in range(4):
            ps, pk = newps()
            for k in range(8):
                mm(ps[:, 0:256], hT[:, k, tb * 128:(tb + 1) * 128], wt[:, k, 256:512], k == 0, k == 7, [wk, hk[k]], [pk])
            tt("dve", kh_tm[:, tb, :], ps[:, 0:256], Ebl[:, tb, :], ALU.mult, [pk, ("Ebl", tb // 2)], [("kh_tm", tb)])
        wt, wk = wload("a_w_in", 512, 512)
        for tb in range(4):
            ps, pk = newps()
            for k in range(8):
                mm(ps[:, :], hT[:, k, tb * 128:(tb + 1) * 128], wt[:, k, :], k == 0, k == 7, [wk, hk[k]], [pk])
            cp("act", v_tm[:, tb, :], ps[:, :], [pk], [("v_tm", tb)])
        wt, wk = wload("a_w_in", 1024, 512)
        for h in range(4):
            ps, pk = newps()
            for k in range(8):
                mm(ps[:, :], wt[:, k, h * 128:(h + 1) * 128], hT[:, k, :], k == 0, k == 7, [wk, hk[k]], [pk])
            act(sr[:, h, :], ps[:, :], AF.Silu, [pk], [("sr", h)])
        wt, wk = wload("a_w_in", 1552, 512)
        for h in range(4):
            ps, pk = newps()
            for k in range(8):
                mm(ps[:, :], wt[:, k, h * 128:(h + 1) * 128], hT[:, k, :], k == 0, k == 7, [wk, hk[k]], [pk])
            act(mqT[:, h, :], ps[:, :], AF.Copy, [pk], [("mqT", h)], scale=float(128 ** -0.5))
        for tb in range(4):
            bs = slice(tb * 128, (tb + 1) * 128)
            ps, pk = newps()
            for h in range(4):
                mm(ps[:, h * 128:(h + 1) * 128], ktT[:, h, bs], qtT[:, h, bs], True, True, [("ktT", h), ("qtT", h)], [pk])
            tt("dve", A_sb[:, :], ps[:, :], cst["mask4"][:], ALU.mult, [pk, "c_mask4"], ["A_sb"])
            pso, pko = newps()
            for h in range(4):
                mm(pso[:, h * 128:(h + 1) * 128], v_tm[:, tb, h * 128:(h + 1) * 128], A_sb[:, h * 128:(h + 1) * 128],
                   True, False, [("v_tm", tb), "A_sb"], [pko])
                mm(pso[:, h * 128:(h + 1) * 128], Sbf[:, h, :], qtT[:, h, bs], False, True, ["Sbf", ("qtT", h)], [pko])
            cp("act", o_sb[:, :], pso[:, :], [pko], ["o_sb"])
            act(osq[:, :], pso[:, :], AF.Square, [pko], ["osq"])
            ps2, pk2 = newps()
            mm(ps2[:, :], cst["onesv"][:], osq[:, :], True, True, ["c_onesv", "osq"], [pk2])
            rsqrt_eps(orst[:, :], ps2[:, :], [pk2], "rstd")
            stt("dve", o_sb[:, :], o_sb[:, :], ghead[:, 0:1], orst[:, :], ALU.mult, ALU.mult, ["o_sb", "rstd", "ghead"],
                ["o_sb"])
            tt("dve", catT[:, 0:4, bs], o_sb[:, :].rearrange("p (h t) -> p h t", h=4), sr[:, :, bs], ALU.mult,
               ["o_sb"] + [("sr", h) for h in range(4)], [("catT", h) for h in range(4)])
            ps3, pk3 = newps()
            for h in range(4):
                mm(ps3[0:64, h * 128:(h + 1) * 128], kh_tm[:, tb, h * 64:(h + 1) * 64], v_tm[:, tb, h * 128:(h + 1) * 128],
                   True, True, [("kh_tm", tb), ("v_tm", tb)], [pk3])
            for h in range(4):
                stt("dve", S[:, h, :], S[:, h, :], Eb[:, h, tb * 128 + 127:tb * 128 + 128], ps3[0:64, h * 128:(h + 1) * 128],
                    ALU.mult, ALU.add, ["S", ("Eb", h), pk3], ["S"])
            cp("act", Sbf[:, :, :], S[:, :, :], ["S"], ["Sbf"])
        mem_attend(0)
        outproj("a_w_out")
        if dbg:
            for k in range(8):
                pass
        ffn(0, tix == 0)
        for k in range(8):
            p.dma("pool", xT_d[k * 128:(k + 1) * 128, t0:t0 + TT], xT[:, k, :], reads=[xk[k]], writes=[("xT_d", tix, k)])
        rms_fm(xT, xk, 6, TT, hT, "hT")
        wt, wk = wload("w_kv", 0, 512)
        wt2, wk2 = wload("w_kv", 512, 256)
        for tb in range(4):
            ps, pk = newps()
            for k in range(8):
                mm(ps[:, :], hT[:, k, tb * 128:(tb + 1) * 128], wt[:, k, :], k == 0, k == 7, [wk, hk[k]], [pk])
            cp("act", kvst[:, 0:512], ps[:, :], [pk], ["kvst"])
            ps, pk = newps()
            for k in range(8):
                mm(ps[:, 0:256], hT[:, k, tb * 128:(tb + 1) * 128], wt2[:, k, 0:256], k == 0, k == 7, [wk2, hk[k]], [pk])
            cp("dve", kvst[:, 512:768], ps[:, 0:256], [pk], ["kvst"])
            p.dma("pool", kvtm_d[t0 + tb * 128:t0 + (tb + 1) * 128, :], kvst[:, :], reads=["kvst"],
                  writes=[("kvtm_d", tix, tb)])
        for br in range(2):
            for g in range(2):
                c0 = (2 + 2 * br) * 128 + g * 64
                wsrc, wsk = (wt, wk) if c0 < 512 else (wt2, wk2)
                cc = c0 if c0 < 512 else c0 - 512
                ps, pk = newps()
                for k in range(8):
                    mm(ps[0:64, :], wsrc[:, k, cc:cc + 64], hT[:, k, :], k == 0, k == 7, [wsk, hk[k]], [pk])
                cp("act", kaT_sb[:, br * 2 + g, :], ps[0:64, :], [pk], [("kaT_sb", br * 2 + g)])
                p.dma("pool", kaT_d[br, g, :, t0:t0 + TT], kaT_sb[:, br * 2 + g, :], reads=[("kaT_sb", br * 2 + g)],
                      writes=[("kaT_d", br, g, tix)])

    p.barrier()
    p.aoff = markA
    flat = p.sb("flat", [128, 2048], F32)
    peb = p.sb("peb", [128, 2, 2048], F32)
    flatT = p.sb("flatT", [128, 16, 128], BF16)
    u_sb = p.sb("u_sb", [128, 2, 128], F32)
    t_sb = p.sb("t_sb", [128, 2, 128], F32)
    gT = p.sb("gT", [128, 2, 128], BF16)
    kc_sb = p.sb("kc_sb", [64, 128], BF16)
    vc_sb = p.sb("vc_sb", [128, 64], BF16)
    p.dma("sp", peb[:, :, :], pe_d[:, :, :], writes=["peb"])
    for which in range(2):
        w1t, w1k = wload("w_ck1" if which == 0 else "w_cv1", 0, 256, krows=2048)
        w2t, w2k = wload("w_ck2" if which == 0 else "w_cv2", 0, 64, krows=256)
        for g in range(2):
            c0 = which * 128 + g * 64
            for nt in range(NCT):
                nn = min(128, NCMP - nt * 128)
                src = bass.AP(tensor=kvtm_d.tensor, offset=(16 * 128 * nt) * 768 + c0,
                              ap=[[16 * 768, nn], [768, 32], [1, 64]])
                p.dma("sp", flat[0:nn, :].rearrange("p (a b) -> p a b", a=32), src, writes=["flat"])
                tt("dve", flat[0:nn, :], flat[0:nn, :], peb[0:nn, which, :], ALU.add, ["flat", "peb"], ["flat"])
                for half in range(4):
                    ps, pk = newps()
                    for cc in range(4):
                        c = half * 4 + cc
                        tr(ps[:, cc * 128:cc * 128 + nn], flat[0:nn, c * 128:(c + 1) * 128], ["flat"], [pk], n=nn)
                    cp("act" if half % 2 else "dve", flatT[:, half * 4:half * 4 + 4, 0:nn],
                       ps[:, :].rearrange("p (a b) -> p a b", a=4)[:, :, 0:nn], [pk], [("flatT", half)])
                for hc in range(2):
                    ps, pk = newps()
                    for c in range(16):
                        mm(ps[:, 0:nn], w1t[:, c, hc * 128:(hc + 1) * 128], flatT[:, c, 0:nn], c == 0, c == 15,
                           [w1k, ("flatT", c // 4)], [pk])
                    uu, tq = u_sb[:, hc, 0:nn], t_sb[:, hc, 0:nn]
                    cp("act", uu, ps[:, 0:nn], [pk], [("u", hc)])
                    act(tq, ps[:, 0:nn], AF.Square, [pk], [("t", hc)])
                    ts("dve", tq, tq, 0.044715, 1.0, ALU.mult, ALU.add, [("t", hc)], [("t", hc)])
                    tt("dve", tq, tq, uu, ALU.mult, [("t", hc), ("u", hc)], [("t", hc)])
                    act(tq, tq, AF.Tanh, [("t", hc)], [("t", hc)], scale=0.7978845608028654)
                    stt("dve", tq, tq, 1.0, uu, ALU.add, ALU.mult, [("t", hc), ("u", hc)], [("t", hc)])
                    act(gT[:, hc, 0:nn], tq, AF.Copy, [("t", hc)], [("gT", hc)], scale=0.5)
                ps, pk = newps()
                if which == 0:
                    for hc in range(2):
                        mm(ps[0:64, 0:nn], w2t[:, hc, 0:64], gT[:, hc, 0:nn], hc == 0, hc == 1, [w2k, ("gT", hc)], [pk])
                    cp("act", kc_sb[:, 0:nn], ps[0:64, 0:nn], [pk], ["kc_sb"])
                    p.dma("pool", kcT_d[g, :, nt * 128:nt * 128 + nn], kc_sb[:, 0:nn], reads=["kc_sb"],
                          writes=[("kcT_d", g, nt)])
                else:
                    for hc in range(2):
                        mm(ps[0:nn, 0:64], gT[:, hc, 0:nn], w2t[:, hc, 0:64], hc == 0, hc == 1, [w2k, ("gT", hc)], [pk])
                    cp("act", vc_sb[0:nn, :], ps[0:nn, 0:64], [pk], ["vc_sb"])
                    p.dma("pool", vc_d[g, nt * 128:nt * 128 + nn, :], vc_sb[0:nn, :], reads=["vc_sb"],
                          writes=[("vc_d", g, nt)])

    p.barrier()
    p.dma("sp", wgl[:, :, :], WB["wgp"][:, :].rearrange("(k p) f -> p k f", p=128), writes=["wgl"])
    for tix in range(NT):
        t0 = tix * TT
        for k in range(8):
            p.dma("sp", xT[:, k, :], xT_d[k * 128:(k + 1) * 128, t0:t0 + TT], writes=[xk[k]])
        rms_fm(xT, xk, 1, TT, hT, "hT")
        wt, wk = wload("b_w_in", 0, 512)
        for h in range(8):
            ps, pk = newps()
            for k in range(8):
                mm(ps[0:64, :], wt[:, k, h * 64:(h + 1) * 64], hT[:, k, :], k == 0, k == 7, [wk, hk[k]], [pk])
            dst = (qtT if h < 4 else ktT)[:, h % 4, :]
            act(dst, ps[0:64, :], AF.Copy, [pk], [("qh", h)], scale=0.125)
            p.dma("pool", qT_d[h, :, t0:t0 + TT], dst, reads=[("qh", h)], writes=[("qT_d", h, tix)])
        ps, pk = newps()
        for k in range(8):
            mm(ps[0:24, :], wgl[:, k, :], hT[:, k, :], k == 0, k == 7, ["wgl", hk[k]], [pk])
        act(gat_sb[0:24, :], ps[0:24, :], AF.Sigmoid, [pk], ["gat_sb"])
        p.dma("pool", gatesT_d[:, t0:t0 + TT], gat_sb[0:24, :], reads=["gat_sb"], writes=[("gatesT_d", tix)])
        wt, wk = wload("b_w_in", 536, 512)
        for h in range(4):
            ps, pk = newps()
            for k in range(8):
                mm(ps[:, :], wt[:, k, h * 128:(h + 1) * 128], hT[:, k, :], k == 0, k == 7, [wk, hk[k]], [pk])
            act(mqT[:, h, :], ps[:, :], AF.Copy, [pk], [("mqT", h)], scale=float(128 ** -0.5))
        mem_attend(1)
        for h in range(4):
            p.dma("pool", mT_d[h * 128:(h + 1) * 128, t0:t0 + TT], catT[:, 4 + h, :], reads=[("catT", 4 + h)],
                  writes=[("mT_d", h, tix)])

    p.barrier()
    p.aoff = markA
    NC = {k: p.sb("n_" + k, list(v.shape), BF16 if v.dtype != np.float32 else F32) for k, v in ncst.items()
          if k not in ("augk", "augkc", "augq")}
    KsT = p.sb("KsT", [68, T], BF16)
    KwT = p.sb("KwT", [68, T], BF16)
    KcT = p.sb("KcT", [68, NCP], BF16)
    VsA = p.sb("VsA", [128, NKB, 65], BF16)
    VwA = p.sb("VwA", [128, NKB, 65], BF16)
    VcA = p.sb("VcA", [128, NCT, 65], BF16)
    Qaug = [p.sb("Qaug%d" % i, [68, 512], BF16) for i in range(2)]
    gt = [p.sb("gt%d" % i, [65, 3, 512], F32) for i in range(2)]
    PcT = p.sb("PcT", [128, NCT, 512], BF16)
    PsT = [p.sb("PsT%d" % i, [128, 512], BF16) for i in range(3)]
    impt = p.sb("impt", [128, 128], F32)
    impw = p.sb("impw", [128, 128], F32)
    m8 = p.sb("m8", [128, 16], F32)
    selneg = p.sb("selneg", [128, 128], F32)
    selT4 = p.sb("selT4", [128, 512], BF16)
    rz4 = p.sb("rz4", [128, 4], F32)
    oT_sb = p.sb("oT_sb", [65, 512], F32)
    Rrow = p.sb("Rrow", [65, 512], F32)
    otmp = p.sb("otmp", [64, 512], F32)
    ocT = [p.sb("ocT%d" % i, [64, 512], F32) for i in range(2)]
    ocTb = [p.sb("ocTb%d" % i, [64, 512], BF16) for i in range(2)]
    print("arena words P4", p.aoff)
    for k in NC:
        v = ncst_d[k]
        p.dma("sp", NC[k], v, writes=["n_" + k])
    st["psmod"] = 4
    st["ps"] = 0
    BIG = 1.0e9

    def fin_branch(pv, pvk, br, qi, first):
        cp("act", oT_sb[:, :], pv[0:65, :], [pvk], ["oT_sb"])
        ts("dve", Rrow[64:65, :], oT_sb[64:65, :], 1e-30, None, ALU.max, None, ["oT_sb"], ["Rrow"])
        p.op("dve", lambda e: e.reciprocal(out=Rrow[64:65, :], in_=Rrow[64:65, :]), ["Rrow"], ["Rrow"])
        tt("dve", Rrow[64:65, :], Rrow[64:65, :], gt[qi][64:65, br, :], ALU.mult, ["Rrow", ("gt", qi)], ["Rrow"])
        prb, prbk = newps()
        mm(prb[0:64, :], NC["ones1"][64:65, 0:64], Rrow[64:65, :], True, True, ["n_ones1", "Rrow"], [prbk])
        if first:
            tt("dve", ocT[qi][:, :], oT_sb[0:64, :], prb[0:64, :], ALU.mult, ["oT_sb", prbk], [("ocT", qi)])
        else:
            tt("dve", otmp[:, :], oT_sb[0:64, :], prb[0:64, :], ALU.mult, ["oT_sb", prbk], ["otmp"])
            tt("pool", ocT[qi][:, :], ocT[qi][:, :], otmp[:, :], ALU.add, ["otmp", ("ocT", qi)], [("ocT", qi)])

    def unit(Kt, kkey, Va, vkey, kb, Q, qkey, pv, pvk, first, last, extra):
        ps, pk = newps()
        mm(ps[:, :], Kt[:, kb * 128:(kb + 1) * 128], Q[:, :], True, len(extra) == 0, [kkey, qkey], [pk])
        for ei, (l_ap, r_ap, rk) in enumerate(extra):
            mm(ps[:, :], l_ap, r_ap, False, ei == len(extra) - 1, rk, [pk])
        pi = st["pt"]
        st["pt"] = (pi + 1) % 3
        act(PsT[pi][:, :], ps[:, :], AF.Exp, [pk], [("PsT", pi)])
        mm(pv[0:65, :], Va[:, kb, :], PsT[pi][:, :], first, last, [("PsT", pi), vkey], [pvk])

    for g in range(2):
        p.dma("sp", KsT[0:64, :], kaT_d[0, g, :, :], writes=["KsT"])
        p.dma("sp", KsT[64:68, :], ncst_d["augk"][:, :], writes=["KsT"])
        p.dma("sp", KwT[0:64, :], kaT_d[1, g, :, :], writes=["KwT"])
        p.dma("sp", KwT[64:68, :], ncst_d["augk"][:, :], writes=["KwT"])
        memset("dve", KcT[:, :], 0.0, ["KcT"])
        p.dma("sp", KcT[0:64, 0:NCMP], kcT_d[g, :, 0:NCMP], writes=["KcT"])
        p.dma("sp", KcT[64:68, :], ncst_d["augkc"][:, :], writes=["KcT"])
        memset("dve", VcA[:, :, :], 0.0, ["VcA"])
        for c in range(NCT):
            nn = min(128, NCMP - c * 128)
            p.dma("sp", VcA[0:nn, c, 0:64], vc_d[g, c * 128:c * 128 + nn, :], writes=["VcA"])
        memset("dve", VcA[:, :, 64:65], 1.0, ["VcA"])
        for (Va, vkey, ci) in ((VsA, "VsA", 3), (VwA, "VwA", 5)):
            cc0 = ci * 128 + g * 64
            p.dma("pool", Va[:, :, 0:64], kvtm_d[:, cc0:cc0 + 64].rearrange("(kb p) d -> p kb d", p=128),
                  writes=[vkey])
            memset("dve", Va[:, :, 64:65], 1.0, [vkey])
        for qb in range(NQB):
            t0 = qb * 128
            qi = qb % 2
            Q, qkey = Qaug[qi], ("Q", qi)
            p.dma("sp", Q[0:64, :].rearrange("d (h q) -> d h q", h=4),
                  qT_d[4 * g:4 * g + 4, :, t0:t0 + 128].rearrange("h d q -> d h q"), writes=[qkey])
            p.dma("sp", Q[64:68, :], ncst_d["augq"][g, qb, :, :], writes=[qkey])
            p.dma("sp", gt[qi][64:65, :, :].rearrange("o b (j q) -> o (b j) q", j=4),
                  gatesT_d[12 * g:12 * g + 12, t0:t0 + 128].rearrange("(o r) q -> o r q", o=1), writes=[("gt", qi)])
            ncc = qb // 16 + 1
            for c in range(ncc):
                m = qb - 16 * c
                ps, pk = newps()
                mm(ps[:, :], KcT[:, c * 128:(c + 1) * 128], Q[:, :], True, m > 16, ["KcT", qkey], [pk])
                if m <= 16:
                    mm(ps[:, :], NC["identb"][:, :], NC["mc"][:, m, :], False, True, ["n_identb", "n_mc"], [pk])
                act(PcT[:, c, :], ps[:, :], AF.Exp, [pk], [("PcT", c)])
            pvc, pvck = psum[4], ("ps", 4)
            for c in range(ncc):
                mm(pvc[0:65, :], VcA[:, c, :], PcT[:, c, :], c == 0, c == ncc - 1, [("PcT", c), "VcA"], [pvck])
            for j in range(4):
                pim, pimk = psum[6 + j // 2], ("ps", 6 + j // 2)
                o0 = (j % 2) * 129
                for c in range(ncc):
                    mm(pim[:, o0:o0 + 129], PcT[:, c, j * 128:(j + 1) * 128], NC["ova"][:, c, :], c == 0,
                       c == ncc - 1, [("PcT", c), "n_ova"], [pimk])
            fin_branch(pvc, pvck, 0, qi, True)
            for hb in range(2):
                zv = psum[6 + hb][:, 0:258].rearrange("p (j e) -> p j e", e=129)[:, :, 128]
                ts("dve", rz4[:, 2 * hb:2 * hb + 2], zv, 1e-30, None, ALU.max, None, [("ps", 6 + hb)], ["rz4"])
            p.op("dve", lambda e: e.reciprocal(out=rz4[:, :], in_=rz4[:, :]), ["rz4"], ["rz4"])
            ts("dve", impt[:, :], psum[6][:, 0:128], rz4[:, 0:1], None, ALU.mult, None, [("ps", 6), "rz4"], ["impt"])
            for j in range(1, 4):
                o0 = (j % 2) * 129
                stt("dve", impt[:, :], psum[6 + j // 2][:, o0:o0 + 128], rz4[:, j:j + 1], impt[:, :], ALU.mult, ALU.add,
                    [("ps", 6 + j // 2), "rz4", "impt"], ["impt"])
            f0 = 126 - 2 * qb
            tt("dve", impt[:, :], impt[:, :], NC["ftab"][:, f0:f0 + 128], ALU.add, ["impt", "n_ftab"], ["impt"])
            memset("dve", impt[:, 0:1], BIG, ["impt"])
            p.op("dve", lambda e: e.max(out=m8[:, 0:8], in_=impt[:, :]), ["impt"], ["m8"])
            p.op("dve", lambda e: e.match_replace(out=impw[:, :], in_to_replace=m8[:, 0:8], in_values=impt[:, :],
                                                  imm_value=-3.0e38), ["impt", "m8"], ["impw"])
            p.op("dve", lambda e: e.max(out=m8[:, 8:16], in_=impw[:, :]), ["impw"], ["m8"])
            ts("dve", selneg[:, :], impt[:, :], m8[:, 15:16], None, ALU.is_lt, None, ["impt", "m8"], ["selneg"])
            ts("dve", selneg[:, :], selneg[:, :], -30000.0, None, ALU.mult, None, ["selneg"], ["selneg"])
            pst, pstk = newps()
            tr(pst[:, 0:128], selneg[:, :], ["selneg"], [pstk])
            for j in range(4):
                cp("act" if j % 2 else "pool" if False else "dve", selT4[:, j * 128:(j + 1) * 128], pst[:, 0:128],
                   [pstk], ["selT4"])
            pvs, pvsk = psum[5], ("ps", 5)
            for kb in range(qb + 1):
                extra = [(NC["expm"][:, kb, :], selT4[:, :], ["n_expm", "selT4"])]
                if kb == qb:
                    extra.append((NC["identb"][:, :], NC["mdiag"][:, :], ["n_identb", "n_mdiag"]))
                unit(KsT, "KsT", VsA, "VsA", kb, Q, qkey, pvs, pvsk, kb == 0, kb == qb, extra)
            fin_branch(pvs, pvsk, 1, qi, False)
            pvw, pvwk = psum[4], ("ps", 4)
            kbs = [kb for kb in range(qb - 4, qb + 1) if kb >= 0]
            for kb in kbs:
                extra = []
                if kb == qb - 4:
                    extra.append((NC["identb"][:, :], NC["mfar"][:, :], ["n_identb", "n_mfar"]))
                if kb == qb:
                    extra.append((NC["identb"][:, :], NC["mdiag"][:, :], ["n_identb", "n_mdiag"]))
                unit(KwT, "KwT", VwA, "VwA", kb, Q, qkey, pvw, pvwk, kb == kbs[0], kb == qb, extra)
            fin_branch(pvw, pvwk, 2, qi, False)
            cp("act", ocTb[qi][:, :], ocT[qi][:, :], [("ocT", qi)], [("ocTb", qi)])
            p.dma("pool", oT_d[256 * g:256 * (g + 1), t0:t0 + 128].rearrange("(j d) q -> d j q", j=4),
                  ocTb[qi][:, :].rearrange("d (j q) -> d j q", j=4), reads=[("ocTb", qi)], writes=[("oT_d", g, qb)])
    st["psmod"] = 8
    st["ps"] = 0

    p.barrier()
    fin = []
    for tix in range(NT):
        t0 = tix * TT
        for k in range(8):
            p.dma("sp", xT[:, k, :], xT_d[k * 128:(k + 1) * 128, t0:t0 + TT], writes=[xk[k]])
        for h in range(4):
            p.dma("sp", catT[:, h, :], oT_d[h * 128:(h + 1) * 128, t0:t0 + TT], writes=[ck[h]])
        for h in range(4):
            p.dma("sp", catT[:, 4 + h, :], mT_d[h * 128:(h + 1) * 128, t0:t0 + TT], writes=[ck[4 + h]])
        outproj("b_w_out")
        ffn(1, tix == 0)
        ps, pk = newps()
        for k in range(8):
            j = st["sq"]
            st["sq"] = 1 - j
            act(sqb[j][:, :], xT[:, k, :], AF.Square, [xk[k]], [("sq", j)])
            mm(ps[:, :], cst["onesd"][:], sqb[j][:, :], k == 0, k == 7, [("sq", j), "c_onesd"], [pk])
        rsqrt_eps(rstd[:, :], ps[:, :], [pk], "rstd")
        for k in range(8):
            stt("dve", xT[:, k, :], xT[:, k, :], gv[:, 7, k:k + 1], rstd[:, :], ALU.mult, ALU.mult,
                [xk[k], "rstd", "gv"], [xk[k]])
        for tb in range(4):
            for half in range(2):
                ps, pk = newps()
                for kk in range(4):
                    k = half * 4 + kk
                    tr(ps[:, kk * 128:(kk + 1) * 128], xT[:, k, tb * 128:(tb + 1) * 128], [xk[k]], [pk])
                cp("act" if half else "dve", xin[:, tb % 2, half * 512:(half + 1) * 512], ps[:, :], [pk],
                   [("xin", tb % 2)])
            p.dma("pool", out_d[t0 + tb * 128:t0 + (tb + 1) * 128, :], xin[:, tb % 2, :], reads=[("xin", tb % 2)],
                  writes=[("out", tix, tb)])
            fin.append(("out", tix, tb))
    p.barrier()
    p.op("sp", lambda e: e.nop(), reads=fin)
    nc = p.finalize()
    return nc, p


def nsa_consts(T):
    bf = ml_dtypes.bfloat16
    NCMP = T // 16 - 1
    NCT = (NCMP + 127) // 128
    NCP = NCT * 128
    NKB = T // 128
    NQB = T // 128
    NEG = -30000.0
    c = {}
    c["identb"] = np.eye(128, dtype=np.float32).astype(bf)
    pos = np.arange(T)
    c["augk"] = np.stack([np.ones(T), np.ones(T), pos % 128, pos - pos % 128]).astype(np.float32).astype(bf)
    e = 16 * np.arange(NCP) + 31
    c["augkc"] = np.stack([np.ones(NCP), np.ones(NCP), e % 128, e - e % 128]).astype(np.float32).astype(bf)
    aq = np.zeros((2, NQB, 4, 512), np.float32)
    i = np.arange(128)
    for g in range(2):
        for j in range(4):
            slope = 2.0 ** (-(4 * g + j + 1))
            for qb in range(NQB):
                aq[g, qb, 0, j * 128:(j + 1) * 128] = -slope * i
                aq[g, qb, 1, j * 128:(j + 1) * 128] = -slope * (128 * qb)
                aq[g, qb, 2, j * 128:(j + 1) * 128] = slope
                aq[g, qb, 3, j * 128:(j + 1) * 128] = slope
    c["augq"] = aq.astype(bf)
    kj = np.arange(128)[:, None]
    qi = np.arange(128)[None, :]
    c["mdiag"] = np.tile(np.where(kj > qi, NEG, 0.0), (1, 4)).astype(np.float32).astype(bf)
    c["mfar"] = np.tile(np.where(kj <= qi, NEG, 0.0), (1, 4)).astype(np.float32).astype(bf)
    mc = np.zeros((128, 17, 512), np.float32)
    for m in range(17):
        mc[:, m, :] = np.tile(np.where(16 * kj + 31 - qi > 128 * m, NEG, 0.0), (1, 4))
    c["mc"] = mc.astype(bf)
    n = np.arange(NCP)[:, None]
    jb = np.arange(T // 64)[None, :]
    ov = np.clip(np.minimum(16 * n + 32, 64 * jb + 64) - np.maximum(16 * n, 64 * jb), 0, None) / 32.0
    ov[NCMP:] = 0.0
    ovp = np.zeros((NCP, 128), np.float32)
    ovp[:, :min(128, T // 64)] = ov[:, :128]
    ova = np.zeros((NCP, 129), np.float32)
    ova[:, :128] = ovp
    ova[:NCMP, 128] = 1.0
    c["ova"] = ova.reshape(NCT, 128, 129).transpose(1, 0, 2).astype(bf)
    c["ones1"] = np.ones((65, 64), np.float32)
    ex = np.zeros((128, NKB, 128), np.float32)
    for kb in range(NKB):
        for r in range(2):
            if 2 * kb + r < 128:
                ex[2 * kb + r, kb, r * 64:(r + 1) * 64] = 1.0
    c["expm"] = ex.astype(bf)
    ft = np.zeros((128, 256), np.float32)
    hi = (np.arange(128) >= 64).astype(np.int64)[:, None]
    r = np.arange(256)[None, :] - 126
    ft = np.where((r == hi) | (r == hi - 1), 1.0e9, np.where(r > hi, -1.0e9, 0.0)).astype(np.float32)
    c["ftab"] = ft
    return c


def host_inputs(inputs, b, T):
    f = lambda a: np.ascontiguousarray(a, dtype=np.float32)
    m = {}
    m["x"] = f(inputs["x"][b, :T])
    m["mem"] = f(inputs["mem"][b])
    gl = [inputs["g_mix"][0], inputs["g_mix"][1], inputs["g_ffn"][0], inputs["g_ffn"][1], inputs["g_mem"][0],
          inputs["g_mem"][1], inputs["g_kv"], inputs["g_final"]]
    m["gvec"] = f(np.stack([np.asarray(g).reshape(8, 128).T for g in gl], axis=1))
    cw = np.asarray(inputs["conv_w"])
    m["convw"] = f(cw.reshape(2, 3, NFC, 128).transpose(3, 0, 2, 1))
    m["convb"] = f(np.asarray(inputs["conv_b"]).reshape(2, NFC, 128).transpose(2, 0, 1))
    m["ghead"] = f(np.asarray(inputs["a_g_head"]).reshape(128, 1))
    wa = np.zeros((32, 256), np.float32)
    wa[0:16] = np.asarray(inputs["a_w_alpha"])[0]
    wa[16] = np.asarray(inputs["a_b_alpha"])[0]
    m["walpha"] = wa
    m["a_w_in"] = f(inputs["a_w_in"][0]); m["a_w_out"] = f(inputs["a_w_out"][0])
    perm = [(4 * g + j) * 3 + br for g in range(2) for br in range(3) for j in range(4)]
    m["wgp"] = f(np.asarray(inputs["b_w_in"][0])[:, 512:536][:, perm])
    m["w_mem_kv0"] = f(inputs["w_mem_kv"][0]); m["w_mem_kv1"] = f(inputs["w_mem_kv"][1])
    m["w_up0"] = f(inputs["w_up"][0]); m["w_up1"] = f(inputs["w_up"][1])
    m["w_down0"] = f(inputs["w_down"][0]); m["w_down1"] = f(inputs["w_down"][1])
    m["w_kv"] = f(inputs["w_kv"]); m["b_w_in"] = f(inputs["b_w_in"][0]); m["b_w_out"] = f(inputs["b_w_out"][0])
    for k in ("w_ck1", "w_ck2", "w_cv1", "w_cv2"):
        m[k] = f(inputs[k])
    pe = np.stack([np.asarray(inputs["pe_k"]).reshape(2048), np.asarray(inputs["pe_v"]).reshape(2048)], axis=0)
    m["pe_kv"] = f(np.broadcast_to(pe[None], (128, 2, 2048)))
    for k, v in host_consts().items():
        m["c_" + k] = v
    for k, v in nsa_consts(T).items():
        m["n_" + k] = v
    return m


def kernel(**inputs):
    T = inputs["x"].shape[1]
    nc, _ = build(T)
    in_maps = [host_inputs(inputs, b % 4, T) for b in range(8)]
    res = run_bass_kernel_spmd(nc, in_maps, core_ids=list(range(8)))
    return np.stack([res.results[b]["out"] for b in range(4)], axis=0).astype(np.float32)
```

```python
import numpy as np
from contextlib import ExitStack
import ml_dtypes
import concourse.bass as bass
import concourse.mybir as mybir
from concourse.bass_utils import run_bass_kernel_spmd

F32 = mybir.dt.float32
BF16 = mybir.dt.bfloat16
ALU = mybir.AluOpType
AF = mybir.ActivationFunctionType
AX = mybir.AxisListType

D = 1024
MEM_LEN = 256
FFN = 2816
NFC = FFN // 128
EPS = 1e-6
A_PROJ = 2064
B_PROJ = 1048


class _Op:
    __slots__ = ("eng", "fn", "deps", "signal", "dma", "semid", "target")


class Prog:
    NDMASEM = 24
    EMBED_WAIT = True
    ARENA_WORDS = 51200
    ENGS = ("pe", "act", "dve", "pool", "sp")

    def __init__(self):
        self.nc = bass.Bass("TRN2", target_bir_lowering=False)
        self.es = ExitStack()
        self.ops = []
        self.lastw = {}
        self.rd_eng = {}
        self.rd_dma = {}
        self.dma_rr = 0
        self.dma_last = [None] * self.NDMASEM
        self.dma_cnt = [0] * self.NDMASEM
        self.arena = None
        self.amax = 0

    def dram(self, name, shape, dt, kind="Internal"):
        return self.nc.dram_tensor(name, list(shape), dt, kind=kind).ap()

    def sb(self, name, shape, dt):
        if self.arena is None:
            self.arena = self.es.enter_context(self.nc.sbuf_tensor("arena", [128, self.ARENA_WORDS], F32))
            self.aoff = 0
        n = 1
        for d in shape[1:]:
            n *= d
        words = (n * (4 if dt == F32 else 2) + 31) // 32 * 8
        assert self.aoff + words <= self.ARENA_WORDS, ("SBUF arena overflow", name, self.aoff, words)
        v = self.arena[0:shape[0], self.aoff:self.aoff + words]
        if dt != F32:
            v = v.bitcast(dt)
        v = v[:, 0:n]
        if len(shape) == 3:
            v = v.rearrange("p (a b) -> p a b", a=shape[1])
        elif len(shape) == 4:
            v = v.rearrange("p (a b c) -> p a b c", a=shape[1], b=shape[2])
        self.aoff += words
        self.amax = max(self.amax, self.aoff)
        return v

    def barrier(self):
        deps = set(x for x in self.dma_last if x is not None)
        last = {}
        for i, o in enumerate(self.ops):
            if not o.dma:
                last[o.eng] = i
        deps.update(last.values())
        for eng in self.ENGS:
            i = self.op(eng, lambda e: e.nop())
            self.ops[i].deps = sorted(deps)

    def ps(self, name, shape, dt=F32):
        return self.es.enter_context(self.nc.psum_tensor(name, list(shape), dt))

    def op(self, eng, fn, reads=(), writes=(), dma=False):
        i = len(self.ops)
        deps = set()
        for k in reads:
            w = self.lastw.get(k)
            if w is not None:
                deps.add(w)
        for k in writes:
            w = self.lastw.get(k)
            if w is not None:
                deps.add(w)
            deps.update(self.rd_eng.get(k, {}).values())
            deps.update(self.rd_dma.get(k, ()))
        o = _Op()
        o.eng, o.fn, o.signal, o.dma = eng, fn, False, dma
        o.semid, o.target = None, None
        if dma:
            s = self.dma_rr % self.NDMASEM
            self.dma_rr += 1
            if self.dma_last[s] is not None:
                deps.add(self.dma_last[s])
            self.dma_cnt[s] += 1
            o.semid, o.target = s, 16 * self.dma_cnt[s]
            self.dma_last[s] = i
        for k in reads:
            if dma:
                self.rd_dma.setdefault(k, []).append(i)
            else:
                self.rd_eng.setdefault(k, {})[eng] = i
        for k in writes:
            self.lastw[k] = i
            self.rd_eng[k] = {}
            self.rd_dma[k] = []
        deps.discard(i)
        o.deps = sorted(d for d in deps if not (eng == "pe" and self.ops[d].eng == "pe" and not self.ops[d].dma))
        self.ops.append(o)
        return i

    def dma(self, eng, out, in_, reads=(), writes=(), **kw):
        return self.op(eng, lambda e: e.dma_start(out=out, in_=in_, **kw), reads, writes, dma=True)

    def finalize(self):
        nc = self.nc
        ops = self.ops
        for o in ops:
            for d in o.deps:
                ops[d].signal = True
        cnt = {e: 0 for e in self.ENGS}
        for o in ops:
            if not o.dma and o.signal:
                cnt[o.eng] += 1
                o.target = cnt[o.eng]
        esem = {e: self.es.enter_context(nc.semaphore("s_" + e)) for e in self.ENGS}
        dsem = [self.es.enter_context(nc.semaphore("d%d" % i)) for i in range(self.NDMASEM)]
        self.nwaits = 0

        def body(e, ename):
            waited = {}
            for o in ops:
                if o.eng != ename:
                    continue
                need = {}
                for d in o.deps:
                    pr = ops[d]
                    key = ("d", pr.semid) if pr.dma else ("e", pr.eng)
                    if waited.get(key, 0) >= pr.target:
                        continue
                    need[key] = max(need.get(key, 0), pr.target)
                items = sorted(need.items())
                for key, val in items[:-1] if self.EMBED_WAIT else items:
                    e.wait_ge(dsem[key[1]] if key[0] == "d" else esem[key[1]], val)
                    self.nwaits += 1
                    waited[key] = val
                ins = o.fn(e)
                if self.EMBED_WAIT and items:
                    key, val = items[-1]
                    ins._wait_ge(dsem[key[1]] if key[0] == "d" else esem[key[1]], val)
                    waited[key] = val
                if o.dma:
                    ins.then_inc(dsem[o.semid], 16)
                elif o.signal:
                    ins.then_inc(esem[o.eng], 1)

        with nc.Block() as block:
            @block.tensor
            def _(e):
                body(e, "pe")

            @block.scalar
            def _(e):
                body(e, "act")

            @block.vector
            def _(e):
                body(e, "dve")

            @block.gpsimd
            def _(e):
                body(e, "pool")

            @block.sync
            def _(e):
                body(e, "sp")
        self.es.close()
        return nc


def host_consts():
    c = {}
    c["ident"] = np.eye(128, dtype=np.float32)
    s = np.arange(128)[:, None]
    t = np.arange(128)[None, :]
    c["uinc"] = np.where(s <= t, -1.0 / 16.0, 0.0).astype(np.float32)
    c["urev"] = np.where(s > t, -1.0 / 16.0, 0.0).astype(np.float32)
    c["mask4"] = np.tile((s <= t).astype(np.float32), (1, 4))
    c["onesd"] = np.full((128, 128), 1.0 / 1024.0, dtype=np.float32)
    c["onesv"] = np.full((128, 128), 1.0 / 128.0, dtype=np.float32)
    c["onesb"] = np.ones((128, 128), dtype=ml_dtypes.bfloat16)
    return c


CONST_DT = {"ident": F32, "uinc": F32, "urev": F32, "mask4": F32, "onesd": F32, "onesv": F32, "onesb": BF16}


def build(T, dbg=False):
    TT = 512
    NT = T // TT
    p = Prog()

    def din(name, shape, dt=F32):
        return p.dram(name, shape, dt, kind="ExternalInput")

    def dtmp(name, shape, dt, out=False):
        return p.dram(name, shape, dt, kind="ExternalOutput" if (out and dbg) else "Internal")

    x_d = din("x", [T, D])
    mem_d = din("mem", [MEM_LEN, D])
    gv_d = din("gvec", [128, 8, 8])
    convw_d = din("convw", [128, 2, NFC, 3])
    convb_d = din("convb", [128, 2, NFC])
    ghead_d = din("ghead", [128, 1])
    walpha_d = din("walpha", [32, 256])
    W = {
        "a_w_in": din("a_w_in", [D, A_PROJ]), "a_w_out": din("a_w_out", [D, D]),
        "w_mem_kv0": din("w_mem_kv0", [D, D]), "w_mem_kv1": din("w_mem_kv1", [D, D]),
        "w_up0": din("w_up0", [D, 2 * FFN]), "w_up1": din("w_up1", [D, 2 * FFN]),
        "w_down0": din("w_down0", [FFN, D]), "w_down1": din("w_down1", [FFN, D]),
        "w_kv": din("w_kv", [D, 768]), "b_w_in": din("b_w_in", [D, B_PROJ]), "b_w_out": din("b_w_out", [D, D]),
        "w_ck1": din("w_ck1", [2048, 256]), "w_cv1": din("w_cv1", [2048, 256]),
        "w_ck2": din("w_ck2", [256, 64]), "w_cv2": din("w_cv2", [256, 64]),
        "wgp": din("wgp", [D, 24]),
    }
    pe_d = din("pe_kv", [128, 2, 2048])
    NCMP = T // 16 - 1
    NCT = (NCMP + 127) // 128
    NCP = NCT * 128
    NKB = T // 128
    NQB = T // 128
    ncst = nsa_consts(T)
    ncst_d = {k: din("n_" + k, list(v.shape), BF16 if v.dtype != np.float32 else F32) for k, v in ncst.items()}
    cst_d = {k: din("c_" + k, list(v.shape), CONST_DT[k]) for k, v in host_consts().items()}

    WB = {k: p.dram("wb_" + k, list(v.shape), BF16) for k, v in W.items()}
    xT_d = dtmp("xT", [D, T], F32, out=True)
    kvtm_d = dtmp("kvtm", [T, 768], F32, out=True)
    kaT_d = dtmp("kaT", [2, 2, 64, T], BF16)
    kcT_d = dtmp("kcT", [2, 64, NCP], BF16, out=True)
    vc_d = dtmp("vc", [2, NCP, 64], BF16, out=True)
    qT_d = dtmp("qT", [8, 64, T], BF16)
    gatesT_d = dtmp("gatesT", [24, T], F32)
    mT_d = dtmp("mT", [512, T], BF16, out=True)
    oT_d = dtmp("oT", [512, T], BF16, out=True)
    out_d = p.dram("out", [T, D], F32, kind="ExternalOutput")

    cst = {k: p.sb("k_" + k, list(v.shape), CONST_DT[k]) for k, v in host_consts().items()}
    gv = p.sb("gv", [128, 8, 8], F32)
    convw = p.sb("convw", [128, 2, NFC, 3], F32)
    convb = p.sb("convb", [128, 2, NFC], F32)
    ghead = p.sb("ghead", [128, 1], F32)
    walpha = p.sb("walpha", [32, 256], F32)
    NWB = 4
    wbuf = [p.sb("wbuf%d" % i, [128, 8, 512], BF16) for i in range(NWB)]
    walr = p.sb("walr", [128, 8, 16], BF16)
    memkT = [p.sb("memkT%d" % l, [128, 4, 256], BF16) for l in range(2)]
    memv = [p.sb("memv%d" % l, [128, 2, 512], BF16) for l in range(2)]
    wgl = p.sb("wgl", [128, 8, 24], BF16)
    markA = p.aoff
    xin = p.sb("xin", [128, 2, D], F32)
    xT = p.sb("xT", [128, 8, TT], F32)
    hT = p.sb("hT", [128, 8, TT], BF16)
    sqb = [p.sb("sq%d" % i, [128, TT], F32) for i in range(2)]
    rstd = p.sb("rstd", [128, TT], F32)
    alrT = p.sb("alrT", [32, TT], F32)
    sp_tm = p.sb("sp_tm", [128, 4, 256], F32)
    Ebl = p.sb("Ebl", [128, 4, 256], F32)
    Eb = p.sb("Eb", [64, 4, TT], F32)
    Enb = p.sb("Enb", [64, 4, TT], F32)
    qtT = p.sb("qtT", [64, 4, TT], BF16)
    ktT = p.sb("ktT", [64, 4, TT], BF16)
    kh_tm = p.sb("kh_tm", [128, 4, 256], BF16)
    v_tm = p.sb("v_tm", [128, 4, 512], BF16)
    sr = p.sb("sr", [128, 4, TT], BF16)
    mqT = p.sb("mqT", [128, 4, TT], BF16)
    catT = p.sb("catT", [128, 8, TT], BF16)
    S = p.sb("S", [64, 4, 128], F32)
    Sbf = p.sb("Sbf", [64, 4, 128], BF16)
    A_sb = p.sb("A_sb", [128, 512], BF16)
    o_sb = p.sb("o_sb", [128, 512], F32)
    osq = p.sb("osq", [128, 512], F32)
    orst = rstd
    pT = [p.sb("pT%d" % i, [128, TT], BF16) for i in range(2)]
    rz = rstd
    abuf = [p.sb("abuf%d" % i, [128, TT + 2], F32) for i in range(2)]
    acc = [p.sb("acc%d" % i, [128, TT], F32) for i in range(2)]
    halo = p.sb("halo", [128, NFC, 2], F32)
    hF = p.sb("hF", [128, NFC, TT], BF16)
    kvst = p.sb("kvst", [128, 768], F32)
    kaT_sb = p.sb("kaT_sb", [64, 4, TT], BF16)
    gat_sb = p.sb("gat_sb", [32, TT], F32)
    print("arena words", p.amax)
    psum = [p.ps("ps%d" % i, [128, 512], F32) for i in range(8)]

    st = {"ps": 0, "wb": 0, "sq": 0, "psmod": 8, "pt": 0}

    def newps():
        i = st["ps"] % st["psmod"]
        st["ps"] = (i + 1) % st["psmod"]
        return psum[i], ("ps", i)

    def mm(out, lhsT, rhs, start, stop, r, w):
        p.op("pe", lambda e: e.matmul(out, lhsT=lhsT, rhs=rhs, start=start, stop=stop), r, w)

    def tr(out, in_, r, w, n=128):
        idn = cst["ident"]
        p.op("pe", lambda e: e.transpose(out, in_, idn[0:n, 0:n]), list(r) + ["c_ident"], w)

    def act(out, in_, func, r, w, **kw):
        p.op("act", lambda e: e.activation(out=out, in_=in_, func=func, **kw), r, w)

    def cp(eng, out, in_, r, w):
        if eng == "act":
            p.op("act", lambda e: e.copy(out=out, in_=in_), r, w)
        else:
            p.op(eng, lambda e: e.tensor_copy(out=out, in_=in_), r, w)

    def tt(eng, out, in0, in1, op, r, w):
        p.op(eng, lambda e: e.tensor_tensor(out=out, in0=in0, in1=in1, op=op), r, w)

    def ts(eng, out, in0, s1, s2, op0, op1, r, w):
        if s2 is None:
            p.op(eng, lambda e: e.tensor_scalar(out=out, in0=in0, scalar1=s1, scalar2=None, op0=op0), r, w)
        else:
            p.op(eng, lambda e: e.tensor_scalar(out=out, in0=in0, scalar1=s1, scalar2=s2, op0=op0, op1=op1), r, w)

    def stt(eng, out, in0, scalar, in1, op0, op1, r, w):
        p.op(eng, lambda e: e.scalar_tensor_tensor(out=out, in0=in0, scalar=scalar, in1=in1, op0=op0, op1=op1), r, w)

    def rsqrt_eps(out, in_, r, wkey):
        act(out, in_, AF.Ln, r, [wkey], bias=EPS, scale=1.0)
        act(out, out, AF.Exp, [wkey], [wkey], scale=-0.5)

    def memset(eng, ap, val, w):
        p.op(eng, lambda e: e.memset(ap, val), (), w)

    for k in cst:
        p.dma("sp", cst[k][:], cst_d[k][:, :], writes=["c_" + k])
    p.dma("sp", gv[:], gv_d[:, :, :], writes=["gv"])
    p.dma("sp", convw[:], convw_d[:, :, :, :], writes=["convw"])
    p.dma("sp", convb[:], convb_d[:, :, :], writes=["convb"])
    p.dma("sp", ghead[:], ghead_d[:, :], writes=["ghead"])
    p.dma("sp", walpha[:], walpha_d[:, :], writes=["walpha"])
    for name, wd in W.items():
        rows = wd.shape[0]
        for k in range(rows // 128):
            p.dma("pool", WB[name][k * 128:(k + 1) * 128, :], wd[k * 128:(k + 1) * 128, :],
                  writes=[("wb", name, k)])

    def wload(name, c0, ncols, krows=D, r0=0):
        nk = krows // 128
        i = st["wb"]
        st["wb"] = (i + 1) % NWB
        src = WB[name][r0:r0 + krows, c0:c0 + ncols].rearrange("(k p) f -> p k f", p=128)
        assert nk * ncols <= 4096
        view = wbuf[i][:, :, :].rearrange("p k f -> p (k f)")[:, 0:nk * ncols].rearrange("p (k f) -> p k f", f=ncols)
        p.dma("sp", view, src,
              reads=[("wb", name, r0 // 128 + k) for k in range(nk)], writes=[("wbuf", i)])
        return view, ("wbuf", i)

    def rms_fm(src, srckeys, gi, N, dst, dstkey, nk=8):
        ps, pk = newps()
        for k in range(nk):
            j = st["sq"]
            st["sq"] = 1 - j
            act(sqb[j][:, 0:N], src[:, k, 0:N], AF.Square, [srckeys[k]], [("sq", j)])
            mm(ps[:, 0:N], cst["onesd"][:], sqb[j][:, 0:N], k == 0, k == nk - 1, [("sq", j), "c_onesd"], [pk])
        rsqrt_eps(rstd[:, 0:N], ps[:, 0:N], [pk], "rstd")
        for k in range(nk):
            stt("dve", dst[:, k, 0:N], src[:, k, 0:N], gv[:, gi, k:k + 1], rstd[:, 0:N], ALU.mult, ALU.mult,
                [srckeys[k], "rstd", "gv"], [(dstkey, k)])

    for l in range(2):
        for mc in range(2):
            p.dma("sp", xin[:, mc, :], mem_d[mc * 128:(mc + 1) * 128, :], writes=[("xin", mc)])
        for k in range(8):
            ps, pk = newps()
            for mc in range(2):
                tr(ps[:, mc * 128:(mc + 1) * 128], xin[:, mc, k * 128:(k + 1) * 128], [("xin", mc)], [pk])
            cp("act", xT[:, k, 0:256], ps[:, 0:256], [pk], [("xT", k)])
        rms_fm(xT, [("xT", k) for k in range(8)], 4 + l, 256, hT, "hT")
        hk = [("hT", k) for k in range(8)]
        wname = "w_mem_kv%d" % l
        wt, wk = wload(wname, 0, 512)
        for h in range(4):
            ps, pk = newps()
            for k in range(8):
                mm(ps[:, 0:256], wt[:, k, h * 128:(h + 1) * 128], hT[:, k, 0:256], k == 0, k == 7, [wk, hk[k]], [pk])
            cp("act", memkT[l][:, h, :], ps[:, 0:256], [pk], [("memkT", l)])
        wt, wk = wload(wname, 512, 512)
        for mc in range(2):
            ps, pk = newps()
            for k in range(8):
                mm(ps[:, :], hT[:, k, mc * 128:(mc + 1) * 128], wt[:, k, :], k == 0, k == 7, [wk, hk[k]], [pk])
            cp("act", memv[l][:, mc, :], ps[:, :], [pk], [("memv", l)])

    xk = [("xT", k) for k in range(8)]
    hk = [("hT", k) for k in range(8)]
    ck = [("catT", k) for k in range(8)]

    def mem_attend(l):
        for h in range(4):
            for mc in range(2):
                ps, pk = newps()
                mm(ps[:, :], memkT[l][:, h, mc * 128:(mc + 1) * 128], mqT[:, h, :], True, True,
                   [("memkT", l), ("mqT", h)], [pk])
                act(pT[mc][:, :], ps[:, :], AF.Exp, [pk], [("pT", mc)])
            pso, pko = newps()
            psz, pkz = newps()
            for mc in range(2):
                mm(pso[:, :], memv[l][:, mc, h * 128:(h + 1) * 128], pT[mc][:, :], mc == 0, mc == 1,
                   [("memv", l), ("pT", mc)], [pko])
            for mc in range(2):
                mm(psz[:, :], cst["onesb"][:], pT[mc][:, :], mc == 0, mc == 1, ["c_onesb", ("pT", mc)], [pkz])
            act(rz[:, :], psz[:, :], AF.Ln, [pkz], ["rstd"])
            act(rz[:, :], rz[:, :], AF.Exp, ["rstd"], ["rstd"], scale=-1.0)
            tt("dve", catT[:, 4 + h, :], pso[:, :], rz[:, :], ALU.mult, [pko, "rstd"], [("catT", 4 + h)])

    def outproj(wname):
        for half in range(2):
            wt, wk = wload(wname, half * 512, 512)
            for dcl in range(4):
                dc = half * 4 + dcl
                ps, pk = newps()
                for k in range(8):
                    mm(ps[:, :], wt[:, k, dcl * 128:(dcl + 1) * 128], catT[:, k, :], k == 0, k == 7, [wk, ck[k]], [pk])
                tt("dve", xT[:, dc, :], xT[:, dc, :], ps[:, :], ALU.add, [pk, xk[dc]], [xk[dc]])

    def ffn(l, first_tile):
        rms_fm(xT, xk, 2 + l, TT, hT, "hT")
        wup = "w_up%d" % l
        for g0 in range(0, NFC, 4):
            ng = min(4, NFC - g0)
            wa, wak = wload(wup, g0 * 128, ng * 128)
            wb_, wbk = wload(wup, FFN + g0 * 128, ng * 128)
            for j in range(ng):
                fc = g0 + j
                psa, pka = newps()
                for k in range(8):
                    mm(psa[:, :], wa[:, k, j * 128:(j + 1) * 128], hT[:, k, :], k == 0, k == 7, [wak, hk[k]], [pka])
                psb, pkb = newps()
                for k in range(8):
                    mm(psb[:, :], wb_[:, k, j * 128:(j + 1) * 128], hT[:, k, :], k == 0, k == 7, [wbk, hk[k]], [pkb])
                i = fc % 2
                ab, ac = abuf[i], acc[i]
                if first_tile:
                    memset("pool", ab[:, 0:2], 0.0, [("abuf", i)])
                else:
                    cp("pool", ab[:, 0:2], halo[:, fc, :], [("halo", fc)], [("abuf", i)])
                cp("act", ab[:, 2:TT + 2], psa[:, :], [pka], [("abuf", i)])
                cp("pool", halo[:, fc, :], ab[:, TT:TT + 2], [("abuf", i)], [("halo", fc)])
                ts("dve", ac[:, :], ab[:, 2:TT + 2], convw[:, l, fc, 2:3], convb[:, l, fc:fc + 1], ALU.mult, ALU.add,
                   [("abuf", i), "convw", "convb"], [("acc", i)])
                stt("dve", ac[:, :], ab[:, 1:TT + 1], convw[:, l, fc, 1:2], ac[:, :], ALU.mult, ALU.add,
                    [("abuf", i), ("acc", i), "convw"], [("acc", i)])
                stt("dve", ac[:, :], ab[:, 0:TT], convw[:, l, fc, 0:1], ac[:, :], ALU.mult, ALU.add,
                    [("abuf", i), ("acc", i), "convw"], [("acc", i)])
                act(ac[:, :], ac[:, :], AF.Silu, [("acc", i)], [("acc", i)])
                tt("dve", hF[:, fc, :], ac[:, :], psb[:, :], ALU.mult, [("acc", i), pkb], [("hF", fc)])
        wdn = "w_down%d" % l
        for half in range(2):
            pss = [newps() for _ in range(4)]
            for g0 in range(0, NFC, 4):
                ng = min(4, NFC - g0)
                wt, wk = wload(wdn, half * 512, 512, krows=ng * 128, r0=g0 * 128)
                for dcl in range(4):
                    for j in range(ng):
                        fc = g0 + j
                        mm(pss[dcl][0][:, :], wt[:, j, dcl * 128:(dcl + 1) * 128], hF[:, fc, :], fc == 0, fc == NFC - 1,
                           [wk, ("hF", fc)], [pss[dcl][1]])
            for dcl in range(4):
                dc = half * 4 + dcl
                tt("dve", xT[:, dc, :], xT[:, dc, :], pss[dcl][0][:, :], ALU.add, [pss[dcl][1], xk[dc]], [xk[dc]])

    memset("dve", alrT[:, :], 1.0, ["alrT"])
    memset("dve", S[:, :, :], 0.0, ["S"])
    memset("dve", Sbf[:, :, :], 0.0, ["Sbf"])
    for tix in range(NT):
        t0 = tix * TT
        for tb in range(4):
            xi = tb % 2
            p.dma("sp", xin[:, xi, :], x_d[t0 + tb * 128:t0 + (tb + 1) * 128, :], writes=[("xin", xi)])
            for half in range(2):
                ps, pk = newps()
                for kk in range(4):
                    k = half * 4 + kk
                    tr(ps[:, kk * 128:(kk + 1) * 128], xin[:, xi, k * 128:(k + 1) * 128], [("xin", xi)], [pk])
                cp("act" if half else "dve", xT[:, half * 4:half * 4 + 4, tb * 128:(tb + 1) * 128],
                   ps[:, :].rearrange("p (a b) -> p a b", a=4), [pk], [xk[half * 4 + kk] for kk in range(4)])
        rms_fm(xT, xk, 0, TT, hT, "hT")
        p.dma("sp", walr[:, :, :], WB["a_w_in"][:, 1536:1552].rearrange("(k p) f -> p k f", p=128),
              reads=[("wb", "a_w_in", k) for k in range(8)], writes=["walr"])
        ps, pk = newps()
        for k in range(8):
            mm(ps[0:16, :], walr[:, k, :], hT[:, k, :], k == 0, k == 7, ["walr", hk[k]], [pk])
        cp("act", alrT[0:16, :], ps[0:16, :], [pk], ["alrT"])
        for tb2 in range(2):
            ps, pk = newps()
            for j in range(2):
                tb = tb2 * 2 + j
                mm(ps[:, j * 256:(j + 1) * 256], alrT[:, tb * 128:(tb + 1) * 128], walpha[:, :], True, True,
                   ["alrT", "walpha"], [pk])
            act(sp_tm[:, tb2 * 2:tb2 * 2 + 2, :], ps[:, :].rearrange("p (a b) -> p a b", a=2), AF.Exp, [pk],
                [("sp_tm", tb2)], scale=-1.0)
            act(sp_tm[:, tb2 * 2:tb2 * 2 + 2, :], sp_tm[:, tb2 * 2:tb2 * 2 + 2, :], AF.Ln, [("sp_tm", tb2)],
                [("sp_tm", tb2)], bias=1.0)
        for tb2 in range(2):
            ps, pk = newps()
            for j in range(2):
                tb = tb2 * 2 + j
                mm(ps[:, j * 256:(j + 1) * 256], cst["urev"][:], sp_tm[:, tb, :], True, True,
                   ["c_urev", ("sp_tm", tb2)], [pk])
            act(Ebl[:, tb2 * 2:tb2 * 2 + 2, :], ps[:, :].rearrange("p (a b) -> p a b", a=2), AF.Exp, [pk],
                [("Ebl", tb2)])
        for h in range(4):
            ps, pk = newps()
            for tb in range(4):
                mm(ps[0:64, tb * 128:(tb + 1) * 128], sp_tm[:, tb, h * 64:(h + 1) * 64], cst["uinc"][:], True, True,
                   ["c_uinc", ("sp_tm", tb // 2)], [pk])
            act(Eb[:, h, :], ps[0:64, :], AF.Exp, [pk], [("Eb", h)])
            act(Enb[:, h, :], ps[0:64, :], AF.Exp, [pk], [("Enb", h)], scale=-1.0)
        wt, wk = wload("a_w_in", 0, 512)
        for h in range(4):
            ps, pk = newps()
            for k in range(8):
                mm(ps[0:64, :], wt[:, k, h * 64:(h + 1) * 64], hT[:, k, :], k == 0, k == 7, [wk, hk[k]], [pk])
            stt("dve", qtT[:, h, :], ps[0:64, :], 0.125, Eb[:, h, :], ALU.mult, ALU.mult, [pk, ("Eb", h)], [("qtT", h)])
            ps, pk = newps()
            for k in range(8):
                mm(ps[0:64, :], wt[:, k, 256 + h * 64:256 + (h + 1) * 64], hT[:, k, :], k == 0, k == 7, [wk, hk[k]], [pk])
            tt("dve", ktT[:, h, :], ps[0:64, :], Enb[:, h, :], ALU.mult, [pk, ("Enb", h)], [("ktT", h)])
        for tb in range(4):
            ps, pk = newps()
            for k in range(8):
                mm(ps[:, 0:256], hT[:, k, tb * 128:(tb + 1) * 128], wt[:, k, 256:512], k == 0, k == 7, [wk, hk[k]], [pk])
            tt("dve", kh_tm[:, tb, :], ps[:, 0:256], Ebl[:, tb, :], ALU.mult, [pk, ("Ebl", tb // 2)], [("kh_tm", tb)])
        wt, wk = wload("a_w_in", 512, 512)
        for tb in range(4):
            ps, pk = newps()
            for k in range(8):
                mm(ps[:, :], hT[:, k, tb * 128:(tb + 1) * 128], wt[:, k, :], k == 0, k == 7, [wk, hk[k]], [pk])
            cp("act", v_tm[:, tb, :], ps[:, :], [pk], [("v_tm", tb)])
        wt, wk = wload("a_w_in", 1024, 512)
        for h in range(4):
            ps, pk = newps()
            for k in range(8):
                mm(ps[:, :], wt[:, k, h * 128:(h + 1) * 128], hT[:, k, :], k == 0, k == 7, [wk, hk[k]], [pk])
            act(sr[:, h, :], ps[:, :], AF.Silu, [pk], [("sr", h)])
        wt, wk = wload("a_w_in", 1552, 512)
        for h in range(4):
            ps, pk = newps()
            for k in range(8):
                mm(ps[:, :], wt[:, k, h * 128:(h + 1) * 128], hT[:, k, :], k == 0, k == 7, [wk, hk[k]], [pk])
            act(mqT[:, h, :], ps[:, :], AF.Copy, [pk], [("mqT", h)], scale=float(128 ** -0.5))
        for tb in range(4):
            bs = slice(tb * 128, (tb + 1) * 128)
            ps, pk = newps()
            for h in range(4):
                mm(ps[:, h * 128:(h + 1) * 128], ktT[:, h, bs], qtT[:, h, bs], True, True, [("ktT", h), ("qtT", h)], [pk])
            tt("dve", A_sb[:, :], ps[:, :], cst["mask4"][:], ALU.mult, [pk, "c_mask4"], ["A_sb"])
            pso, pko = newps()
            for h in range(4):
                mm(pso[:, h * 128:(h + 1) * 128], v_tm[:, tb, h * 128:(h + 1) * 128], A_sb[:, h * 128:(h + 1) * 128],
                   True, False, [("v_tm", tb), "A_sb"], [pko])
                mm(pso[:, h * 128:(h + 1) * 128], Sbf[:, h, :], qtT[:, h, bs], False, True, ["Sbf", ("qtT", h)], [pko])
            cp("act", o_sb[:, :], pso[:, :], [pko], ["o_sb"])
            act(osq[:, :], pso[:, :], AF.Square, [pko], ["osq"])
            ps2, pk2 = newps()
            mm(ps2[:, :], cst["onesv"][:], osq[:, :], True, True, ["c_onesv", "osq"], [pk2])
            rsqrt_eps(orst[:, :], ps2[:, :], [pk2], "rstd")
            stt("dve", o_sb[:, :], o_sb[:, :], ghead[:, 0:1], orst[:, :], ALU.mult, ALU.mult, ["o_sb", "rstd", "ghead"],
                ["o_sb"])
            tt("dve", catT[:, 0:4, bs], o_sb[:, :].rearrange("p (h t) -> p h t", h=4), sr[:, :, bs], ALU.mult,
               ["o_sb"] + [("sr", h) for h in range(4)], [("catT", h) for h in range(4)])
            ps3, pk3 = newps()
            for h in range(4):
                mm(ps3[0:64, h * 128:(h + 1) * 128], kh_tm[:, tb, h * 64:(h + 1) * 64], v_tm[:, tb, h * 128:(h + 1) * 128],
                   True, True, [("kh_tm", tb), ("v_tm", tb)], [pk3])
            for h in range(4):
                stt("dve", S[:, h, :], S[:, h, :], Eb[:, h, tb * 128 + 127:tb * 128 + 128], ps3[0:64, h * 128:(h + 1) * 128],
                    ALU.mult, ALU.add, ["S", ("Eb", h), pk3], ["S"])
            cp("act", Sbf[:, :, :], S[:, :, :], ["S"], ["Sbf"])
        mem_attend(0)
        outproj("a_w_out")
        if dbg:
            for k in range(8):
                pass
        ffn(0, tix == 0)
        for k in range(8):
            p.dma("pool", xT_d[k * 128:(k + 1) * 128, t0:t0 + TT], xT[:, k, :], reads=[xk[k]], writes=[("xT_d", tix, k)])
        rms_fm(xT, xk, 6, TT, hT, "hT")
        wt, wk = wload("w_kv", 0, 512)
        wt2, wk2 = wload("w_kv", 512, 256)
        for tb in range(4):
            ps, pk = newps()
            for k in range(8):
                mm(ps[:, :], hT[:, k, tb * 128:(tb + 1) * 128], wt[:, k, :], k == 0, k == 7, [wk, hk[k]], [pk])
            cp("act", kvst[:, 0:512], ps[:, :], [pk], ["kvst"])
            ps, pk = newps()
            for k in range(8):
                mm(ps[:, 0:256], hT[:, k, tb * 128:(tb + 1) * 128], wt2[:, k, 0:256], k == 0, k == 7, [wk2, hk[k]], [pk])
            cp("dve", kvst[:, 512:768], ps[:, 0:256], [pk], ["kvst"])
            p.dma("pool", kvtm_d[t0 + tb * 128:t0 + (tb + 1) * 128, :], kvst[:, :], reads=["kvst"],
                  writes=[("kvtm_d", tix, tb)])
        for br in range(2):
            for g in range(2):
                c0 = (2 + 2 * br) * 128 + g * 64
                wsrc, wsk = (wt, wk) if c0 < 512 else (wt2, wk2)
                cc = c0 if c0 < 512 else c0 - 512
                ps, pk = newps()
                for k in range(8):
                    mm(ps[0:64, :], wsrc[:, k, cc:cc + 64], hT[:, k, :], k == 0, k == 7, [wsk, hk[k]], [pk])
                cp("act", kaT_sb[:, br * 2 + g, :], ps[0:64, :], [pk], [("kaT_sb", br * 2 + g)])
                p.dma("pool", kaT_d[br, g, :, t0:t0 + TT], kaT_sb[:, br * 2 + g, :], reads=[("kaT_sb", br * 2 + g)],
                      writes=[("kaT_d", br, g, tix)])

    p.barrier()
    p.aoff = markA
    flat = p.sb("flat", [128, 2048], F32)
    peb = p.sb("peb", [128, 2, 2048], F32)
    flatT = p.sb("flatT", [128, 16, 128], BF16)
    u_sb = p.sb("u_sb", [128, 2, 128], F32)
    t_sb = p.sb("t_sb", [128, 2, 128], F32)
    gT = p.sb("gT", [128, 2, 128], BF16)
    kc_sb = p.sb("kc_sb", [64, 128], BF16)
    vc_sb = p.sb("vc_sb", [128, 64], BF16)
    p.dma("sp", peb[:, :, :], pe_d[:, :, :], writes=["peb"])
    for which in range(2):
        w1t, w1k = wload("w_ck1" if which == 0 else "w_cv1", 0, 256, krows=2048)
        w2t, w2k = wload("w_ck2" if which == 0 else "w_cv2", 0, 64, krows=256)
        for g in range(2):
            c0 = which * 128 + g * 64
            for nt in range(NCT):
                nn = min(128, NCMP - nt * 128)
                src = bass.AP(tensor=kvtm_d.tensor, offset=(16 * 128 * nt) * 768 + c0,
                              ap=[[16 * 768, nn], [768, 32], [1, 64]])
                p.dma("sp", flat[0:nn, :].rearrange("p (a b) -> p a b", a=32), src, writes=["flat"])
                tt("dve", flat[0:nn, :], flat[0:nn, :], peb[0:nn, which, :], ALU.add, ["flat", "peb"], ["flat"])
                for half in range(4):
                    ps, pk = newps()
                    for cc in range(4):
                        c = half * 4 + cc
                        tr(ps[:, cc * 128:cc * 128 + nn], flat[0:nn, c * 128:(c + 1) * 128], ["flat"], [pk], n=nn)
                    cp("act" if half % 2 else "dve", flatT[:, half * 4:half * 4 + 4, 0:nn],
                       ps[:, :].rearrange("p (a b) -> p a b", a=4)[:, :, 0:nn], [pk], [("flatT", half)])
                for hc in range(2):
                    ps, pk = newps()
                    for c in range(16):
                        mm(ps[:, 0:nn], w1t[:, c, hc * 128:(hc + 1) * 128], flatT[:, c, 0:nn], c == 0, c == 15,
                           [w1k, ("flatT", c // 4)], [pk])
                    uu, tq = u_sb[:, hc, 0:nn], t_sb[:, hc, 0:nn]
                    cp("act", uu, ps[:, 0:nn], [pk], [("u", hc)])
                    act(tq, ps[:, 0:nn], AF.Square, [pk], [("t", hc)])
                    ts("dve", tq, tq, 0.044715, 1.0, ALU.mult, ALU.add, [("t", hc)], [("t", hc)])
                    tt("dve", tq, tq, uu, ALU.mult, [("t", hc), ("u", hc)], [("t", hc)])
                    act(tq, tq, AF.Tanh, [("t", hc)], [("t", hc)], scale=0.7978845608028654)
                    stt("dve", tq, tq, 1.0, uu, ALU.add, ALU.mult, [("t", hc), ("u", hc)], [("t", hc)])
                    act(gT[:, hc, 0:nn], tq, AF.Copy, [("t", hc)], [("gT", hc)], scale=0.5)
                ps, pk = newps()
                if which == 0:
                    for hc in range(2):
                        mm(ps[0:64, 0:nn], w2t[:, hc, 0:64], gT[:, hc, 0:nn], hc == 0, hc == 1, [w2k, ("gT", hc)], [pk])
                    cp("act", kc_sb[:, 0:nn], ps[0:64, 0:nn], [pk], ["kc_sb"])
                    p.dma("pool", kcT_d[g, :, nt * 128:nt * 128 + nn], kc_sb[:, 0:nn], reads=["kc_sb"],
                          writes=[("kcT_d", g, nt)])
                else:
                    for hc in range(2):
                        mm(ps[0:nn, 0:64], gT[:, hc, 0:nn], w2t[:, hc, 0:64], hc == 0, hc == 1, [w2k, ("gT", hc)], [pk])
                    cp("act", vc_sb[0:nn, :], ps[0:nn, 0:64], [pk], ["vc_sb"])
                    p.dma("pool", vc_d[g, nt * 128:nt * 128 + nn, :], vc_sb[0:nn, :], reads=["vc_sb"],
                          writes=[("vc_d", g, nt)])

    p.barrier()
    p.dma("sp", wgl[:, :, :], WB["wgp"][:, :].rearrange("(k p) f -> p k f", p=128), writes=["wgl"])
    for tix in range(NT):
        t0 = tix * TT
        for k in range(8):
            p.dma("sp", xT[:, k, :], xT_d[k * 128:(k + 1) * 128, t0:t0 + TT], writes=[xk[k]])
        rms_fm(xT, xk, 1, TT, hT, "hT")
        wt, wk = wload("b_w_in", 0, 512)
        for h in range(8):
            ps, pk = newps()
            for k in range(8):
                mm(ps[0:64, :], wt[:, k, h * 64:(h + 1) * 64], hT[:, k, :], k == 0, k == 7, [wk, hk[k]], [pk])
            dst = (qtT if h < 4 else ktT)[:, h % 4, :]
            act(dst, ps[0:64, :], AF.Copy, [pk], [("qh", h)], scale=0.125)
            p.dma("pool", qT_d[h, :, t0:t0 + TT], dst, reads=[("qh", h)], writes=[("qT_d", h, tix)])
        ps, pk = newps()
        for k in range(8):
            mm(ps[0:24, :], wgl[:, k, :], hT[:, k, :], k == 0, k == 7, ["wgl", hk[k]], [pk])
        act(gat_sb[0:24, :], ps[0:24, :], AF.Sigmoid, [pk], ["gat_sb"])
        p.dma("pool", gatesT_d[:, t0:t0 + TT], gat_sb[0:24, :], reads=["gat_sb"], writes=[("gatesT_d", tix)])
        wt, wk = wload("b_w_in", 536, 512)
        for h in range(4):
            ps, pk = newps()
            for k in range(8):
                mm(ps[:, :], wt[:, k, h * 128:(h + 1) * 128], hT[:, k, :], k == 0, k == 7, [wk, hk[k]], [pk])
            act(mqT[:, h, :], ps[:, :], AF.Copy, [pk], [("mqT", h)], scale=float(128 ** -0.5))
        mem_attend(1)
        for h in range(4):
            p.dma("pool", mT_d[h * 128:(h + 1) * 128, t0:t0 + TT], catT[:, 4 + h, :], reads=[("catT", 4 + h)],
                  writes=[("mT_d", h, tix)])

    p.barrier()
    p.aoff = markA
    NC = {k: p.sb("n_" + k, list(v.shape), BF16 if v.dtype != np.float32 else F32) for k, v in ncst.items()
          if k not in ("augk", "augkc", "augq")}
    KsT = p.sb("KsT", [68, T], BF16)
    KwT = p.sb("KwT", [68, T], BF16)
    KcT = p.sb("KcT", [68, NCP], BF16)
    VsA = p.sb("VsA", [128, NKB, 65], BF16)
    VwA = p.sb("VwA", [128, NKB, 65], BF16)
    VcA = p.sb("VcA", [128, NCT, 65], BF16)
    Qaug = [p.sb("Qaug%d" % i, [68, 512], BF16) for i in range(2)]
    gt = [p.sb("gt%d" % i, [65, 3, 512], F32) for i in range(2)]
    PcT = p.sb("PcT", [128, NCT, 512], BF16)
    PsT = [p.sb("PsT%d" % i, [128, 512], BF16) for i in range(3)]
    impt = p.sb("impt", [128, 128], F32)
    impw = p.sb("impw", [128, 128], F32)
    m8 = p.sb("m8", [128, 16], F32)
    selneg = p.sb("selneg", [128, 128], F32)
    selT4 = [p.sb("selT4_%d" % i, [128, 512], BF16) for i in range(2)]
    rz4 = p.sb("rz4", [128, 4], F32)
    oT_sb = p.sb("oT_sb", [65, 512], F32)
    Rrow = p.sb("Rrow", [65, 512], F32)
    otmp = p.sb("otmp", [64, 512], F32)
    ocT = [p.sb("ocT%d" % i, [64, 512], F32) for i in range(2)]
    ocTb = [p.sb("ocTb%d" % i, [64, 512], BF16) for i in range(2)]
    print("arena words P4", p.aoff)
    for k in NC:
        v = ncst_d[k]
        p.dma("sp", NC[k], v, writes=["n_" + k])
    st["psmod"] = 4
    st["ps"] = 0
    BIG = 1.0e9

    def fin_branch(pv, pvk, br, qi, first):
        cp("act", oT_sb[:, :], pv[0:65, :], [pvk], ["oT_sb"])
        ts("dve", Rrow[64:65, :], oT_sb[64:65, :], 1e-30, None, ALU.max, None, ["oT_sb"], ["Rrow"])
        act(Rrow[64:65, :], Rrow[64:65, :], AF.Ln, ["Rrow"], ["Rrow"])
        act(Rrow[64:65, :], Rrow[64:65, :], AF.Exp, ["Rrow"], ["Rrow"], scale=-1.0)
        tt("dve", Rrow[64:65, :], Rrow[64:65, :], gt[qi][64:65, br, :], ALU.mult, ["Rrow", ("gt", qi)], ["Rrow"])
        prb, prbk = newps()
        mm(prb[0:64, :], NC["ones1"][64:65, 0:64], Rrow[64:65, :], True, True, ["n_ones1", "Rrow"], [prbk])
        if first:
            tt("dve", ocT[qi][:, :], oT_sb[0:64, :], prb[0:64, :], ALU.mult, ["oT_sb", prbk], [("ocT", qi)])
        else:
            tt("dve", otmp[:, :], oT_sb[0:64, :], prb[0:64, :], ALU.mult, ["oT_sb", prbk], ["otmp"])
            tt("pool", ocT[qi][:, :], ocT[qi][:, :], otmp[:, :], ALU.add, ["otmp", ("ocT", qi)], [("ocT", qi)])

    def unit_s(u):
        ps, pk = newps()
        kb = u["kb"]
        extra = u["extra"]
        mm(ps[:, :], u["K"][:, kb * 128:(kb + 1) * 128], u["Q"][:, :], True, len(extra) == 0, [u["kkey"], u["qkey"]], [pk])
        for ei, (l_ap, r_ap, rk) in enumerate(extra):
            mm(ps[:, :], l_ap, r_ap, False, ei == len(extra) - 1, rk, [pk])
        pi = st["pt"]
        st["pt"] = (pi + 1) % 3
        act(PsT[pi][:, :], ps[:, :], AF.Exp, [pk], [("PsT", pi)])
        return pi

    def unit_pv(u, pi):
        mm(u["pv"][0:65, :], u["V"][:, u["kb"], :], PsT[pi][:, :], u["first"], u["last"], [("PsT", pi), u["vkey"]],
           [u["pvk"]])
        if u["after"] is not None:
            u["after"]()

    def stage_A(g, qb):
        t0 = qb * 128
        qi = qb % 2
        Q, qkey = Qaug[qi], ("Q", qi)
        p.dma("sp", Q[0:64, :].rearrange("d (h q) -> d h q", h=4),
              qT_d[4 * g:4 * g + 4, :, t0:t0 + 128].rearrange("h d q -> d h q"), writes=[qkey])
        p.dma("sp", Q[64:68, :], ncst_d["augq"][g, qb, :, :], writes=[qkey])
        p.dma("sp", gt[qi][64:65, :, :].rearrange("o b (j q) -> o (b j) q", j=4),
              gatesT_d[12 * g:12 * g + 12, t0:t0 + 128].rearrange("(o r) q -> o r q", o=1), writes=[("gt", qi)])
        ncc = qb // 16 + 1
        for c in range(ncc):
            m = qb - 16 * c
            ps, pk = newps()
            mm(ps[:, :], KcT[:, c * 128:(c + 1) * 128], Q[:, :], True, m > 16, ["KcT", qkey], [pk])
            if m <= 16:
                mm(ps[:, :], NC["identb"][:, :], NC["mc"][:, m, :], False, True, ["n_identb", "n_mc"], [pk])
            act(PcT[:, c, :], ps[:, :], AF.Exp, [pk], [("PcT", c)])
        pvc, pvck = psum[4], ("ps", 4)
        for c in range(ncc):
            mm(pvc[0:65, :], VcA[:, c, :], PcT[:, c, :], c == 0, c == ncc - 1, [("PcT", c), "VcA"], [pvck])
        for j in range(4):
            pim, pimk = psum[6 + j // 2], ("ps", 6 + j // 2)
            o0 = (j % 2) * 129
            for c in range(ncc):
                mm(pim[:, o0:o0 + 129], PcT[:, c, j * 128:(j + 1) * 128], NC["ova"][:, c, :], c == 0,
                   c == ncc - 1, [("PcT", c), "n_ova"], [pimk])
        fin_branch(pvc, pvck, 0, qi, True)
        for hb in range(2):
            zv = psum[6 + hb][:, 0:258].rearrange("p (j e) -> p j e", e=129)[:, :, 128]
            ts("dve", rz4[:, 2 * hb:2 * hb + 2], zv, 1e-30, None, ALU.max, None, [("ps", 6 + hb)], ["rz4"])
        p.op("dve", lambda e: e.reciprocal(out=rz4[:, :], in_=rz4[:, :]), ["rz4"], ["rz4"])
        ts("dve", impt[:, :], psum[6][:, 0:128], rz4[:, 0:1], None, ALU.mult, None, [("ps", 6), "rz4"], ["impt"])
        for j in range(1, 4):
            o0 = (j % 2) * 129
            stt("dve", impt[:, :], psum[6 + j // 2][:, o0:o0 + 128], rz4[:, j:j + 1], impt[:, :], ALU.mult, ALU.add,
                [("ps", 6 + j // 2), "rz4", "impt"], ["impt"])
        f0 = 126 - 2 * qb
        tt("dve", impt[:, :], impt[:, :], NC["ftab"][:, f0:f0 + 128], ALU.add, ["impt", "n_ftab"], ["impt"])
        memset("dve", impt[:, 0:1], BIG, ["impt"])
        p.op("dve", lambda e: e.max(out=m8[:, 0:8], in_=impt[:, :]), ["impt"], ["m8"])
        p.op("dve", lambda e: e.match_replace(out=impw[:, :], in_to_replace=m8[:, 0:8], in_values=impt[:, :],
                                              imm_value=-3.0e38), ["impt", "m8"], ["impw"])
        p.op("dve", lambda e: e.max(out=m8[:, 8:16], in_=impw[:, :]), ["impw"], ["m8"])
        ts("dve", selneg[:, :], impt[:, :], m8[:, 15:16], None, ALU.is_lt, None, ["impt", "m8"], ["selneg"])
        ts("dve", selneg[:, :], selneg[:, :], -30000.0, None, ALU.mult, None, ["selneg"], ["selneg"])
        pst, pstk = newps()
        tr(pst[:, 0:128], selneg[:, :], ["selneg"], [pstk])
        for j in range(4):
            cp("act" if j % 2 else "dve", selT4[qi][:, j * 128:(j + 1) * 128], pst[:, 0:128], [pstk], [("selT4", qi)])

    def stage_B(g, qb):
        t0 = qb * 128
        qi = qb % 2
        Q, qkey = Qaug[qi], ("Q", qi)
        pvs, pvsk = psum[5], ("ps", 5)
        pvw, pvwk = psum[4], ("ps", 4)
        units = []
        for kb in range(qb + 1):
            extra = [(NC["expm"][:, kb, :], selT4[qi][:, :], ["n_expm", ("selT4", qi)])]
            if kb == qb:
                extra.append((NC["identb"][:, :], NC["mdiag"][:, :], ["n_identb", "n_mdiag"]))
            units.append(dict(K=KsT, kkey="KsT", V=VsA, vkey="VsA", kb=kb, Q=Q, qkey=qkey, extra=extra, pv=pvs, pvk=pvsk,
                              first=kb == 0, last=kb == qb,
                              after=(lambda: fin_branch(pvs, pvsk, 1, qi, False)) if kb == qb else None))
        kbs = [kb for kb in range(qb - 4, qb + 1) if kb >= 0]

        def done():
            fin_branch(pvw, pvwk, 2, qi, False)
            cp("act", ocTb[qi][:, :], ocT[qi][:, :], [("ocT", qi)], [("ocTb", qi)])
            p.dma("pool", oT_d[256 * g:256 * (g + 1), t0:t0 + 128].rearrange("(j d) q -> d j q", j=4),
                  ocTb[qi][:, :].rearrange("d (j q) -> d j q", j=4), reads=[("ocTb", qi)], writes=[("oT_d", g, qb)])

        for kb in kbs:
            extra = []
            if kb == qb - 4:
                extra.append((NC["identb"][:, :], NC["mfar"][:, :], ["n_identb", "n_mfar"]))
            if kb == qb:
                extra.append((NC["identb"][:, :], NC["mdiag"][:, :], ["n_identb", "n_mdiag"]))
            units.append(dict(K=KwT, kkey="KwT", V=VwA, vkey="VwA", kb=kb, Q=Q, qkey=qkey, extra=extra, pv=pvw, pvk=pvwk,
                              first=kb == kbs[0], last=kb == qb, after=done if kb == qb else None))
        pend = None
        for u in units:
            pi = unit_s(u)
            if pend is not None:
                unit_pv(*pend)
            pend = (u, pi)
        unit_pv(*pend)

    for g in range(2):
        p.dma("sp", KsT[0:64, :], kaT_d[0, g, :, :], writes=["KsT"])
        p.dma("sp", KsT[64:68, :], ncst_d["augk"][:, :], writes=["KsT"])
        p.dma("sp", KwT[0:64, :], kaT_d[1, g, :, :], writes=["KwT"])
        p.dma("sp", KwT[64:68, :], ncst_d["augk"][:, :], writes=["KwT"])
        memset("dve", KcT[:, :], 0.0, ["KcT"])
        p.dma("sp", KcT[0:64, 0:NCMP], kcT_d[g, :, 0:NCMP], writes=["KcT"])
        p.dma("sp", KcT[64:68, :], ncst_d["augkc"][:, :], writes=["KcT"])
        memset("dve", VcA[:, :, :], 0.0, ["VcA"])
        for c in range(NCT):
            nn = min(128, NCMP - c * 128)
            p.dma("sp", VcA[0:nn, c, 0:64], vc_d[g, c * 128:c * 128 + nn, :], writes=["VcA"])
        memset("dve", VcA[:, :, 64:65], 1.0, ["VcA"])
        for (Va, vkey, ci) in ((VsA, "VsA", 3), (VwA, "VwA", 5)):
            cc0 = ci * 128 + g * 64
            p.dma("pool", Va[:, :, 0:64], kvtm_d[:, cc0:cc0 + 64].rearrange("(kb p) d -> p kb d", p=128),
                  writes=[vkey])
            memset("dve", Va[:, :, 64:65], 1.0, [vkey])
        stage_A(g, 0)
        for qb in range(NQB):
            if qb + 1 < NQB:
                stage_A(g, qb + 1)
            stage_B(g, qb)
    st["psmod"] = 8
    st["ps"] = 0

    p.barrier()
    fin = []
    for tix in range(NT):
        t0 = tix * TT
        for k in range(8):
            p.dma("sp", xT[:, k, :], xT_d[k * 128:(k + 1) * 128, t0:t0 + TT], writes=[xk[k]])
        for h in range(4):
            p.dma("sp", catT[:, h, :], oT_d[h * 128:(h + 1) * 128, t0:t0 + TT], writes=[ck[h]])
        for h in range(4):
            p.dma("sp", catT[:, 4 + h, :], mT_d[h * 128:(h + 1) * 128, t0:t0 + TT], writes=[ck[4 + h]])
        outproj("b_w_out")
        ffn(1, tix == 0)
        ps, pk = newps()
        for k in range(8):
            j = st["sq"]
            st["sq"] = 1 - j
            act(sqb[j][:, :], xT[:, k, :], AF.Square, [xk[k]], [("sq", j)])
            mm(ps[:, :], cst["onesd"][:], sqb[j][:, :], k == 0, k == 7, [("sq", j), "c_onesd"], [pk])
        rsqrt_eps(rstd[:, :], ps[:, :], [pk], "rstd")
        for k in range(8):
            stt("dve", xT[:, k, :], xT[:, k, :], gv[:, 7, k:k + 1], rstd[:, :], ALU.mult, ALU.mult,
                [xk[k], "rstd", "gv"], [xk[k]])
        for tb in range(4):
            for half in range(2):
                ps, pk = newps()
                for kk in range(4):
                    k = half * 4 + kk
                    tr(ps[:, kk * 128:(kk + 1) * 128], xT[:, k, tb * 128:(tb + 1) * 128], [xk[k]], [pk])
                cp("act" if half else "dve", xin[:, tb % 2, half * 512:(half + 1) * 512], ps[:, :], [pk],
                   [("xin", tb % 2)])
            p.dma("pool", out_d[t0 + tb * 128:t0 + (tb + 1) * 128, :], xin[:, tb % 2, :], reads=[("xin", tb % 2)],
                  writes=[("out", tix, tb)])
            fin.append(("out", tix, tb))
    p.barrier()
    p.op("sp", lambda e: e.nop(), reads=fin)
    nc = p.finalize()
    return nc, p


def nsa_consts(T):
    bf = ml_dtypes.bfloat16
    NCMP = T // 16 - 1
    NCT = (NCMP + 127) // 128
    NCP = NCT * 128
    NKB = T // 128
    NQB = T // 128
    NEG = -30000.0
    c = {}
    c["identb"] = np.eye(128, dtype=np.float32).astype(bf)
    pos = np.arange(T)
    c["augk"] = np.stack([np.ones(T), np.ones(T), pos % 128, pos - pos % 128]).astype(np.float32).astype(bf)
    e = 16 * np.arange(NCP) + 31
    c["augkc"] = np.stack([np.ones(NCP), np.ones(NCP), e % 128, e - e % 128]).astype(np.float32).astype(bf)
    aq = np.zeros((2, NQB, 4, 512), np.float32)
    i = np.arange(128)
    for g in range(2):
        for j in range(4):
            slope = 2.0 ** (-(4 * g + j + 1))
            for qb in range(NQB):
                aq[g, qb, 0, j * 128:(j + 1) * 128] = -slope * i
                aq[g, qb, 1, j * 128:(j + 1) * 128] = -slope * (128 * qb)
                aq[g, qb, 2, j * 128:(j + 1) * 128] = slope
                aq[g, qb, 3, j * 128:(j + 1) * 128] = slope
    c["augq"] = aq.astype(bf)
    kj = np.arange(128)[:, None]
    qi = np.arange(128)[None, :]
    c["mdiag"] = np.tile(np.where(kj > qi, NEG, 0.0), (1, 4)).astype(np.float32).astype(bf)
    c["mfar"] = np.tile(np.where(kj <= qi, NEG, 0.0), (1, 4)).astype(np.float32).astype(bf)
    mc = np.zeros((128, 17, 512), np.float32)
    for m in range(17):
        mc[:, m, :] = np.tile(np.where(16 * kj + 31 - qi > 128 * m, NEG, 0.0), (1, 4))
    c["mc"] = mc.astype(bf)
    n = np.arange(NCP)[:, None]
    jb = np.arange(T // 64)[None, :]
    ov = np.clip(np.minimum(16 * n + 32, 64 * jb + 64) - np.maximum(16 * n, 64 * jb), 0, None) / 32.0
    ov[NCMP:] = 0.0
    ovp = np.zeros((NCP, 128), np.float32)
    ovp[:, :min(128, T // 64)] = ov[:, :128]
    ova = np.zeros((NCP, 129), np.float32)
    ova[:, :128] = ovp
    ova[:NCMP, 128] = 1.0
    c["ova"] = ova.reshape(NCT, 128, 129).transpose(1, 0, 2).astype(bf)
    c["ones1"] = np.ones((65, 64), np.float32)
    ex = np.zeros((128, NKB, 128), np.float32)
    for kb in range(NKB):
        for r in range(2):
            if 2 * kb + r < 128:
                ex[2 * kb + r, kb, r * 64:(r + 1) * 64] = 1.0
    c["expm"] = ex.astype(bf)
    ft = np.zeros((128, 256), np.float32)
    hi = (np.arange(128) >= 64).astype(np.int64)[:, None]
    r = np.arange(256)[None, :] - 126
    ft = np.where((r == hi) | (r == hi - 1), 1.0e9, np.where(r > hi, -1.0e9, 0.0)).astype(np.float32)
    c["ftab"] = ft
    return c


def host_inputs(inputs, b, T):
    f = lambda a: np.ascontiguousarray(a, dtype=np.float32)
    m = {}
    m["x"] = f(inputs["x"][b, :T])
    m["mem"] = f(inputs["mem"][b])
    gl = [inputs["g_mix"][0], inputs["g_mix"][1], inputs["g_ffn"][0], inputs["g_ffn"][1], inputs["g_mem"][0],
          inputs["g_mem"][1], inputs["g_kv"], inputs["g_final"]]
    m["gvec"] = f(np.stack([np.asarray(g).reshape(8, 128).T for g in gl], axis=1))
    cw = np.asarray(inputs["conv_w"])
    m["convw"] = f(cw.reshape(2, 3, NFC, 128).transpose(3, 0, 2, 1))
    m["convb"] = f(np.asarray(inputs["conv_b"]).reshape(2, NFC, 128).transpose(2, 0, 1))
    m["ghead"] = f(np.asarray(inputs["a_g_head"]).reshape(128, 1))
    wa = np.zeros((32, 256), np.float32)
    wa[0:16] = np.asarray(inputs["a_w_alpha"])[0]
    wa[16] = np.asarray(inputs["a_b_alpha"])[0]
    m["walpha"] = wa
    m["a_w_in"] = f(inputs["a_w_in"][0]); m["a_w_out"] = f(inputs["a_w_out"][0])
    perm = [(4 * g + j) * 3 + br for g in range(2) for br in range(3) for j in range(4)]
    m["wgp"] = f(np.asarray(inputs["b_w_in"][0])[:, 512:536][:, perm])
    m["w_mem_kv0"] = f(inputs["w_mem_kv"][0]); m["w_mem_kv1"] = f(inputs["w_mem_kv"][1])
    m["w_up0"] = f(inputs["w_up"][0]); m["w_up1"] = f(inputs["w_up"][1])
    m["w_down0"] = f(inputs["w_down"][0]); m["w_down1"] = f(inputs["w_down"][1])
    m["w_kv"] = f(inputs["w_kv"]); m["b_w_in"] = f(inputs["b_w_in"][0]); m["b_w_out"] = f(inputs["b_w_out"][0])
    for k in ("w_ck1", "w_ck2", "w_cv1", "w_cv2"):
        m[k] = f(inputs[k])
    pe = np.stack([np.asarray(inputs["pe_k"]).reshape(2048), np.asarray(inputs["pe_v"]).reshape(2048)], axis=0)
    m["pe_kv"] = f(np.broadcast_to(pe[None], (128, 2, 2048)))
    for k, v in host_consts().items():
        m["c_" + k] = v
    for k, v in nsa_consts(T).items():
        m["n_" + k] = v
    return m


def kernel(**inputs):
    T = inputs["x"].shape[1]
    nc, _ = build(T)
    in_maps = [host_inputs(inputs, b % 4, T) for b in range(8)]
    res = run_bass_kernel_spmd(nc, in_maps, core_ids=list(range(8)))
    return np.stack([res.results[b]["out"] for b in range(4)], axis=0).astype(np.float32)
```

```python
import numpy as np
from contextlib import ExitStack
import ml_dtypes
import concourse.bass as bass
import concourse.mybir as mybir
from concourse.bass_utils import run_bass_kernel_spmd

F32 = mybir.dt.float32
BF16 = mybir.dt.bfloat16
ALU = mybir.AluOpType
AF = mybir.ActivationFunctionType
AX = mybir.AxisListType

D = 1024
MEM_LEN = 256
FFN = 2816
NFC = FFN // 128
EPS = 1e-6
A_PROJ = 2064
B_PROJ = 1048


class _Op:
    __slots__ = ("eng", "fn", "deps", "signal", "dma", "semid", "target")


class Prog:
    NDMASEM = 24
    EMBED_WAIT = True
    ARENA_WORDS = 51200
    ENGS = ("pe", "act", "dve", "pool", "sp")

    def __init__(self):
        self.nc = bass.Bass("TRN2", target_bir_lowering=False)
        self.es = ExitStack()
        self.ops = []
        self.lastw = {}
        self.rd_eng = {}
        self.rd_dma = {}
        self.dma_rr = 0
        self.dma_last = [None] * self.NDMASEM
        self.dma_cnt = [0] * self.NDMASEM
        self.arena = None
        self.amax = 0

    def dram(self, name, shape, dt, kind="Internal"):
        return self.nc.dram_tensor(name, list(shape), dt, kind=kind).ap()

    def sb(self, name, shape, dt):
        if self.arena is None:
            self.arena = self.es.enter_context(self.nc.sbuf_tensor("arena", [128, self.ARENA_WORDS], F32))
            self.aoff = 0
        n = 1
        for d in shape[1:]:
            n *= d
        words = (n * (4 if dt == F32 else 2) + 31) // 32 * 8
        assert self.aoff + words <= self.ARENA_WORDS, ("SBUF arena overflow", name, self.aoff, words)
        v = self.arena[0:shape[0], self.aoff:self.aoff + words]
        if dt != F32:
            v = v.bitcast(dt)
        v = v[:, 0:n]
        if len(shape) == 3:
            v = v.rearrange("p (a b) -> p a b", a=shape[1])
        elif len(shape) == 4:
            v = v.rearrange("p (a b c) -> p a b c", a=shape[1], b=shape[2])
        self.aoff += words
        self.amax = max(self.amax, self.aoff)
        return v

    def barrier(self):
        deps = set(x for x in self.dma_last if x is not None)
        last = {}
        for i, o in enumerate(self.ops):
            if not o.dma:
                last[o.eng] = i
        deps.update(last.values())
        for eng in self.ENGS:
            i = self.op(eng, lambda e: e.nop())
            self.ops[i].deps = sorted(deps)

    def ps(self, name, shape, dt=F32):
        return self.es.enter_context(self.nc.psum_tensor(name, list(shape), dt))

    def op(self, eng, fn, reads=(), writes=(), dma=False):
        i = len(self.ops)
        deps = set()
        for k in reads:
            w = self.lastw.get(k)
            if w is not None:
                deps.add(w)
        for k in writes:
            w = self.lastw.get(k)
            if w is not None:
                deps.add(w)
            deps.update(self.rd_eng.get(k, {}).values())
            deps.update(self.rd_dma.get(k, ()))
        o = _Op()
        o.eng, o.fn, o.signal, o.dma = eng, fn, False, dma
        o.semid, o.target = None, None
        if dma:
            s = self.dma_rr % self.NDMASEM
            self.dma_rr += 1
            if self.dma_last[s] is not None:
                deps.add(self.dma_last[s])
            self.dma_cnt[s] += 1
            o.semid, o.target = s, 16 * self.dma_cnt[s]
            self.dma_last[s] = i
        for k in reads:
            if dma:
                self.rd_dma.setdefault(k, []).append(i)
            else:
                self.rd_eng.setdefault(k, {})[eng] = i
        for k in writes:
            self.lastw[k] = i
            self.rd_eng[k] = {}
            self.rd_dma[k] = []
        deps.discard(i)
        o.deps = sorted(d for d in deps if not (eng == "pe" and self.ops[d].eng == "pe" and not self.ops[d].dma))
        self.ops.append(o)
        return i

    def dma(self, eng, out, in_, reads=(), writes=(), **kw):
        return self.op(eng, lambda e: e.dma_start(out=out, in_=in_, **kw), reads, writes, dma=True)

    def finalize(self):
        nc = self.nc
        ops = self.ops
        for o in ops:
            for d in o.deps:
                ops[d].signal = True
        cnt = {e: 0 for e in self.ENGS}
        for o in ops:
            if not o.dma and o.signal:
                cnt[o.eng] += 1
                o.target = cnt[o.eng]
        esem = {e: self.es.enter_context(nc.semaphore("s_" + e)) for e in self.ENGS}
        dsem = [self.es.enter_context(nc.semaphore("d%d" % i)) for i in range(self.NDMASEM)]
        self.nwaits = 0

        def body(e, ename):
            waited = {}
            for o in ops:
                if o.eng != ename:
                    continue
                need = {}
                for d in o.deps:
                    pr = ops[d]
                    key = ("d", pr.semid) if pr.dma else ("e", pr.eng)
                    if waited.get(key, 0) >= pr.target:
                        continue
                    need[key] = max(need.get(key, 0), pr.target)
                items = sorted(need.items())
                for key, val in items[:-1] if self.EMBED_WAIT else items:
                    e.wait_ge(dsem[key[1]] if key[0] == "d" else esem[key[1]], val)
                    self.nwaits += 1
                    waited[key] = val
                ins = o.fn(e)
                if self.EMBED_WAIT and items:
                    key, val = items[-1]
                    ins._wait_ge(dsem[key[1]] if key[0] == "d" else esem[key[1]], val)
                    waited[key] = val
                if o.dma:
                    ins.then_inc(dsem[o.semid], 16)
                elif o.signal:
                    ins.then_inc(esem[o.eng], 1)

        with nc.Block() as block:
            @block.tensor
            def _(e):
                body(e, "pe")

            @block.scalar
            def _(e):
                body(e, "act")

            @block.vector
            def _(e):
                body(e, "dve")

            @block.gpsimd
            def _(e):
                body(e, "pool")

            @block.sync
            def _(e):
                body(e, "sp")
        self.es.close()
        return nc


def host_consts():
    c = {}
    c["ident"] = np.eye(128, dtype=np.float32)
    s = np.arange(128)[:, None]
    t = np.arange(128)[None, :]
    c["uinc"] = np.where(s <= t, -1.0 / 16.0, 0.0).astype(np.float32)
    c["urev"] = np.where(s > t, -1.0 / 16.0, 0.0).astype(np.float32)
    c["mask4"] = np.tile((s <= t).astype(np.float32), (1, 4))
    c["onesd"] = np.full((128, 128), 1.0 / 1024.0, dtype=np.float32)
    c["onesv"] = np.full((128, 128), 1.0 / 128.0, dtype=np.float32)
    c["onesb"] = np.ones((128, 128), dtype=ml_dtypes.bfloat16)
    return c


CONST_DT = {"ident": F32, "uinc": F32, "urev": F32, "mask4": F32, "onesd": F32, "onesv": F32, "onesb": BF16}


def build(T, dbg=False):
    TT = 512
    NT = T // TT
    p = Prog()

    def din(name, shape, dt=F32):
        return p.dram(name, shape, dt, kind="ExternalInput")

    def dtmp(name, shape, dt, out=False):
        return p.dram(name, shape, dt, kind="ExternalOutput" if (out and dbg) else "Internal")

    x_d = din("x", [T, D])
    mem_d = din("mem", [MEM_LEN, D])
    gv_d = din("gvec", [128, 8, 8])
    convw_d = din("convw", [128, 2, NFC, 3])
    convb_d = din("convb", [128, 2, NFC])
    ghead_d = din("ghead", [128, 1])
    walpha_d = din("walpha", [32, 256])
    W = {
        "a_w_in": din("a_w_in", [D, A_PROJ]), "a_w_out": din("a_w_out", [D, D]),
        "w_mem_kv0": din("w_mem_kv0", [D, D]), "w_mem_kv1": din("w_mem_kv1", [D, D]),
        "w_up0": din("w_up0", [D, 2 * FFN]), "w_up1": din("w_up1", [D, 2 * FFN]),
        "w_down0": din("w_down0", [FFN, D]), "w_down1": din("w_down1", [FFN, D]),
        "w_kv": din("w_kv", [D, 768]), "b_w_in": din("b_w_in", [D, B_PROJ]), "b_w_out": din("b_w_out", [D, D]),
        "w_ck1": din("w_ck1", [2048, 256]), "w_cv1": din("w_cv1", [2048, 256]),
        "w_ck2": din("w_ck2", [256, 64]), "w_cv2": din("w_cv2", [256, 64]),
        "wgp": din("wgp", [D, 24]),
    }
    pe_d = din("pe_kv", [128, 2, 2048])
    NCMP = T // 16 - 1
    NCT = (NCMP + 127) // 128
    NCP = NCT * 128
    NKB = T // 128
    NQB = T // 128
    ncst = nsa_consts(T)
    ncst_d = {k: din("n_" + k, list(v.shape), BF16 if v.dtype != np.float32 else F32) for k, v in ncst.items()}
    cst_d = {k: din("c_" + k, list(v.shape), CONST_DT[k]) for k, v in host_consts().items()}

    WB = {k: p.dram("wb_" + k, list(v.shape), BF16) for k, v in W.items()}
    xT_d = dtmp("xT", [D, T], F32, out=True)
    kvtm_d = dtmp("kvtm", [T, 768], F32, out=True)
    kaT_d = dtmp("kaT", [2, 2, 64, T], BF16)
    kcT_d = dtmp("kcT", [2, 64, NCP], BF16, out=True)
    vc_d = dtmp("vc", [2, NCP, 64], BF16, out=True)
    qT_d = dtmp("qT", [8, 64, T], BF16)
    gatesT_d = dtmp("gatesT", [24, T], F32)
    mT_d = dtmp("mT", [512, T], BF16, out=True)
    oT_d = dtmp("oT", [512, T], BF16, out=True)
    out_d = p.dram("out", [T, D], F32, kind="ExternalOutput")

    cst = {k: p.sb("k_" + k, list(v.shape), CONST_DT[k]) for k, v in host_consts().items()}
    gv = p.sb("gv", [128, 8, 8], F32)
    convw = p.sb("convw", [128, 2, NFC, 3], F32)
    convb = p.sb("convb", [128, 2, NFC], F32)
    ghead = p.sb("ghead", [128, 1], F32)
    walpha = p.sb("walpha", [32, 256], F32)
    NWB = 4
    wbuf = [p.sb("wbuf%d" % i, [128, 8, 512], BF16) for i in range(NWB)]
    walr = p.sb("walr", [128, 8, 16], BF16)
    memkT = [p.sb("memkT%d" % l, [128, 4, 256], BF16) for l in range(2)]
    memv = [p.sb("memv%d" % l, [128, 2, 512], BF16) for l in range(2)]
    wgl = p.sb("wgl", [128, 8, 24], BF16)
    markA = p.aoff
    xin = p.sb("xin", [128, 2, D], F32)
    xT = p.sb("xT", [128, 8, TT], F32)
    hT = p.sb("hT", [128, 8, TT], BF16)
    sqb = [p.sb("sq%d" % i, [128, TT], F32) for i in range(2)]
    rstd = p.sb("rstd", [128, TT], F32)
    alrT = p.sb("alrT", [32, TT], F32)
    sp_tm = p.sb("sp_tm", [128, 4, 256], F32)
    Ebl = p.sb("Ebl", [128, 4, 256], F32)
    Eb = p.sb("Eb", [64, 4, TT], F32)
    Enb = p.sb("Enb", [64, 4, TT], F32)
    qtT = p.sb("qtT", [64, 4, TT], BF16)
    ktT = p.sb("ktT", [64, 4, TT], BF16)
    kh_tm = p.sb("kh_tm", [128, 4, 256], BF16)
    v_tm = p.sb("v_tm", [128, 4, 512], BF16)
    sr = p.sb("sr", [128, 4, TT], BF16)
    mqT = p.sb("mqT", [128, 4, TT], BF16)
    catT = p.sb("catT", [128, 8, TT], BF16)
    S = p.sb("S", [64, 4, 128], F32)
    Sbf = p.sb("Sbf", [64, 4, 128], BF16)
    A_sb = p.sb("A_sb", [128, 512], BF16)
    o_sb = p.sb("o_sb", [128, 512], F32)
    osq = p.sb("osq", [128, 512], F32)
    orst = rstd
    pT = [p.sb("pT%d" % i, [128, TT], BF16) for i in range(2)]
    rz = rstd
    abuf = [p.sb("abuf%d" % i, [128, TT + 2], F32) for i in range(2)]
    acc = [p.sb("acc%d" % i, [128, TT], F32) for i in range(2)]
    halo = p.sb("halo", [128, NFC, 2], F32)
    hF = p.sb("hF", [128, NFC, TT], BF16)
    kvst = p.sb("kvst", [128, 768], F32)
    kaT_sb = p.sb("kaT_sb", [64, 4, TT], BF16)
    gat_sb = p.sb("gat_sb", [32, TT], F32)
    print("arena words", p.amax)
    psum = [p.ps("ps%d" % i, [128, 512], F32) for i in range(8)]

    st = {"ps": 0, "wb": 0, "sq": 0, "psmod": 8, "pt": 0}

    def newps():
        i = st["ps"] % st["psmod"]
        st["ps"] = (i + 1) % st["psmod"]
        return psum[i], ("ps", i)

    def mm(out, lhsT, rhs, start, stop, r, w):
        p.op("pe", lambda e: e.matmul(out, lhsT=lhsT, rhs=rhs, start=start, stop=stop), r, w)

    def tr(out, in_, r, w, n=128):
        idn = cst["ident"]
        p.op("pe", lambda e: e.transpose(out, in_, idn[0:n, 0:n]), list(r) + ["c_ident"], w)

    def act(out, in_, func, r, w, **kw):
        p.op("act", lambda e: e.activation(out=out, in_=in_, func=func, **kw), r, w)

    def cp(eng, out, in_, r, w):
        if eng == "act":
            p.op("act", lambda e: e.copy(out=out, in_=in_), r, w)
        else:
            p.op(eng, lambda e: e.tensor_copy(out=out, in_=in_), r, w)

    def tt(eng, out, in0, in1, op, r, w):
        p.op(eng, lambda e: e.tensor_tensor(out=out, in0=in0, in1=in1, op=op), r, w)

    def ts(eng, out, in0, s1, s2, op0, op1, r, w):
        if s2 is None:
            p.op(eng, lambda e: e.tensor_scalar(out=out, in0=in0, scalar1=s1, scalar2=None, op0=op0), r, w)
        else:
            p.op(eng, lambda e: e.tensor_scalar(out=out, in0=in0, scalar1=s1, scalar2=s2, op0=op0, op1=op1), r, w)

    def stt(eng, out, in0, scalar, in1, op0, op1, r, w):
        p.op(eng, lambda e: e.scalar_tensor_tensor(out=out, in0=in0, scalar=scalar, in1=in1, op0=op0, op1=op1), r, w)

    def rsqrt_eps(out, in_, r, wkey):
        act(out, in_, AF.Ln, r, [wkey], bias=EPS, scale=1.0)
        act(out, out, AF.Exp, [wkey], [wkey], scale=-0.5)

    def memset(eng, ap, val, w):
        p.op(eng, lambda e: e.memset(ap, val), (), w)

    for k in cst:
        p.dma("sp", cst[k][:], cst_d[k][:, :], writes=["c_" + k])
    p.dma("sp", gv[:], gv_d[:, :, :], writes=["gv"])
    p.dma("sp", convw[:], convw_d[:, :, :, :], writes=["convw"])
    p.dma("sp", convb[:], convb_d[:, :, :], writes=["convb"])
    p.dma("sp", ghead[:], ghead_d[:, :], writes=["ghead"])
    p.dma("sp", walpha[:], walpha_d[:, :], writes=["walpha"])
    for name, wd in W.items():
        rows = wd.shape[0]
        for k in range(rows // 128):
            p.dma("pool", WB[name][k * 128:(k + 1) * 128, :], wd[k * 128:(k + 1) * 128, :],
                  writes=[("wb", name, k)])

    def wload(name, c0, ncols, krows=D, r0=0):
        nk = krows // 128
        i = st["wb"]
        st["wb"] = (i + 1) % NWB
        src = WB[name][r0:r0 + krows, c0:c0 + ncols].rearrange("(k p) f -> p k f", p=128)
        assert nk * ncols <= 4096
        view = wbuf[i][:, :, :].rearrange("p k f -> p (k f)")[:, 0:nk * ncols].rearrange("p (k f) -> p k f", f=ncols)
        p.dma("sp", view, src,
              reads=[("wb", name, r0 // 128 + k) for k in range(nk)], writes=[("wbuf", i)])
        return view, ("wbuf", i)

    def rms_fm(src, srckeys, gi, N, dst, dstkey, nk=8):
        ps, pk = newps()
        for k in range(nk):
            j = st["sq"]
            st["sq"] = 1 - j
            act(sqb[j][:, 0:N], src[:, k, 0:N], AF.Square, [srckeys[k]], [("sq", j)])
            mm(ps[:, 0:N], cst["onesd"][:], sqb[j][:, 0:N], k == 0, k == nk - 1, [("sq", j), "c_onesd"], [pk])
        rsqrt_eps(rstd[:, 0:N], ps[:, 0:N], [pk], "rstd")
        for k in range(nk):
            stt("dve", dst[:, k, 0:N], src[:, k, 0:N], gv[:, gi, k:k + 1], rstd[:, 0:N], ALU.mult, ALU.mult,
                [srckeys[k], "rstd", "gv"], [(dstkey, k)])

    for l in range(2):
        for mc in range(2):
            p.dma("sp", xin[:, mc, :], mem_d[mc * 128:(mc + 1) * 128, :], writes=[("xin", mc)])
        for k in range(8):
            ps, pk = newps()
            for mc in range(2):
                tr(ps[:, mc * 128:(mc + 1) * 128], xin[:, mc, k * 128:(k + 1) * 128], [("xin", mc)], [pk])
            cp("act", xT[:, k, 0:256], ps[:, 0:256], [pk], [("xT", k)])
        rms_fm(xT, [("xT", k) for k in range(8)], 4 + l, 256, hT, "hT")
        hk = [("hT", k) for k in range(8)]
        wname = "w_mem_kv%d" % l
        wt, wk = wload(wname, 0, 512)
        for h in range(4):
            ps, pk = newps()
            for k in range(8):
                mm(ps[:, 0:256], wt[:, k, h * 128:(h + 1) * 128], hT[:, k, 0:256], k == 0, k == 7, [wk, hk[k]], [pk])
            cp("act", memkT[l][:, h, :], ps[:, 0:256], [pk], [("memkT", l)])
        wt, wk = wload(wname, 512, 512)
        for mc in range(2):
            ps, pk = newps()
            for k in range(8):
                mm(ps[:, :], hT[:, k, mc * 128:(mc + 1) * 128], wt[:, k, :], k == 0, k == 7, [wk, hk[k]], [pk])
            cp("act", memv[l][:, mc, :], ps[:, :], [pk], [("memv", l)])

    xk = [("xT", k) for k in range(8)]
    hk = [("hT", k) for k in range(8)]
    ck = [("catT", k) for k in range(8)]

    def mem_attend(l):
        for h in range(4):
            for mc in range(2):
                ps, pk = newps()
                mm(ps[:, :], memkT[l][:, h, mc * 128:(mc + 1) * 128], mqT[:, h, :], True, True,
                   [("memkT", l), ("mqT", h)], [pk])
                act(pT[mc][:, :], ps[:, :], AF.Exp, [pk], [("pT", mc)])
            pso, pko = newps()
            psz, pkz = newps()
            for mc in range(2):
                mm(pso[:, :], memv[l][:, mc, h * 128:(h + 1) * 128], pT[mc][:, :], mc == 0, mc == 1,
                   [("memv", l), ("pT", mc)], [pko])
            for mc in range(2):
                mm(psz[:, :], cst["onesb"][:], pT[mc][:, :], mc == 0, mc == 1, ["c_onesb", ("pT", mc)], [pkz])
            act(rz[:, :], psz[:, :], AF.Ln, [pkz], ["rstd"])
            act(rz[:, :], rz[:, :], AF.Exp, ["rstd"], ["rstd"], scale=-1.0)
            tt("dve", catT[:, 4 + h, :], pso[:, :], rz[:, :], ALU.mult, [pko, "rstd"], [("catT", 4 + h)])

    def outproj(wname):
        for half in range(2):
            wt, wk = wload(wname, half * 512, 512)
            for dcl in range(4):
                dc = half * 4 + dcl
                ps, pk = newps()
                for k in range(8):
                    mm(ps[:, :], wt[:, k, dcl * 128:(dcl + 1) * 128], catT[:, k, :], k == 0, k == 7, [wk, ck[k]], [pk])
                tt("dve", xT[:, dc, :], xT[:, dc, :], ps[:, :], ALU.add, [pk, xk[dc]], [xk[dc]])

    def ffn(l, first_tile):
        rms_fm(xT, xk, 2 + l, TT, hT, "hT")
        wup = "w_up%d" % l
        for g0 in range(0, NFC, 4):
            ng = min(4, NFC - g0)
            wa, wak = wload(wup, g0 * 128, ng * 128)
            wb_, wbk = wload(wup, FFN + g0 * 128, ng * 128)
            for j in range(ng):
                fc = g0 + j
                psa, pka = newps()
                for k in range(8):
                    mm(psa[:, :], wa[:, k, j * 128:(j + 1) * 128], hT[:, k, :], k == 0, k == 7, [wak, hk[k]], [pka])
                psb, pkb = newps()
                for k in range(8):
                    mm(psb[:, :], wb_[:, k, j * 128:(j + 1) * 128], hT[:, k, :], k == 0, k == 7, [wbk, hk[k]], [pkb])
                i = fc % 2
                ab, ac = abuf[i], acc[i]
                if first_tile:
                    memset("pool", ab[:, 0:2], 0.0, [("abuf", i)])
                else:
                    cp("pool", ab[:, 0:2], halo[:, fc, :], [("halo", fc)], [("abuf", i)])
                cp("act", ab[:, 2:TT + 2], psa[:, :], [pka], [("abuf", i)])
                cp("pool", halo[:, fc, :], ab[:, TT:TT + 2], [("abuf", i)], [("halo", fc)])
                ts("dve", ac[:, :], ab[:, 2:TT + 2], convw[:, l, fc, 2:3], convb[:, l, fc:fc + 1], ALU.mult, ALU.add,
                   [("abuf", i), "convw", "convb"], [("acc", i)])
                stt("dve", ac[:, :], ab[:, 1:TT + 1], convw[:, l, fc, 1:2], ac[:, :], ALU.mult, ALU.add,
                    [("abuf", i), ("acc", i), "convw"], [("acc", i)])
                stt("dve", ac[:, :], ab[:, 0:TT], convw[:, l, fc, 0:1], ac[:, :], ALU.mult, ALU.add,
                    [("abuf", i), ("acc", i), "convw"], [("acc", i)])
                act(ac[:, :], ac[:, :], AF.Silu, [("acc", i)], [("acc", i)])
                tt("dve", hF[:, fc, :], ac[:, :], psb[:, :], ALU.mult, [("acc", i), pkb], [("hF", fc)])
        wdn = "w_down%d" % l
        for half in range(2):
            pss = [newps() for _ in range(4)]
            for g0 in range(0, NFC, 4):
                ng = min(4, NFC - g0)
                wt, wk = wload(wdn, half * 512, 512, krows=ng * 128, r0=g0 * 128)
                for dcl in range(4):
                    for j in range(ng):
                        fc = g0 + j
                        mm(pss[dcl][0][:, :], wt[:, j, dcl * 128:(dcl + 1) * 128], hF[:, fc, :], fc == 0, fc == NFC - 1,
                           [wk, ("hF", fc)], [pss[dcl][1]])
            for dcl in range(4):
                dc = half * 4 + dcl
                tt("dve", xT[:, dc, :], xT[:, dc, :], pss[dcl][0][:, :], ALU.add, [pss[dcl][1], xk[dc]], [xk[dc]])

    memset("dve", alrT[:, :], 1.0, ["alrT"])
    memset("dve", S[:, :, :], 0.0, ["S"])
    memset("dve", Sbf[:, :, :], 0.0, ["Sbf"])
    for tix in range(NT):
        t0 = tix * TT
        for tb in range(4):
            xi = tb % 2
            p.dma("sp", xin[:, xi, :], x_d[t0 + tb * 128:t0 + (tb + 1) * 128, :], writes=[("xin", xi)])
            for half in range(2):
                ps, pk = newps()
                for kk in range(4):
                    k = half * 4 + kk
                    tr(ps[:, kk * 128:(kk + 1) * 128], xin[:, xi, k * 128:(k + 1) * 128], [("xin", xi)], [pk])
                cp("act" if half else "dve", xT[:, half * 4:half * 4 + 4, tb * 128:(tb + 1) * 128],
                   ps[:, :].rearrange("p (a b) -> p a b", a=4), [pk], [xk[half * 4 + kk] for kk in range(4)])
        rms_fm(xT, xk, 0, TT, hT, "hT")
        p.dma("sp", walr[:, :, :], WB["a_w_in"][:, 1536:1552].rearrange("(k p) f -> p k f", p=128),
              reads=[("wb", "a_w_in", k) for k in range(8)], writes=["walr"])
        ps, pk = newps()
        for k in range(8):
            mm(ps[0:16, :], walr[:, k, :], hT[:, k, :], k == 0, k == 7, ["walr", hk[k]], [pk])
        cp("act", alrT[0:16, :], ps[0:16, :], [pk], ["alrT"])
        for tb2 in range(2):
            ps, pk = newps()
            for j in range(2):
                tb = tb2 * 2 + j
                mm(ps[:, j * 256:(j + 1) * 256], alrT[:, tb * 128:(tb + 1) * 128], walpha[:, :], True, True,
                   ["alrT", "walpha"], [pk])
            act(sp_tm[:, tb2 * 2:tb2 * 2 + 2, :], ps[:, :].rearrange("p (a b) -> p a b", a=2), AF.Exp, [pk],
                [("sp_tm", tb2)], scale=-1.0)
            act(sp_tm[:, tb2 * 2:tb2 * 2 + 2, :], sp_tm[:, tb2 * 2:tb2 * 2 + 2, :], AF.Ln, [("sp_tm", tb2)],
                [("sp_tm", tb2)], bias=1.0)
        for tb2 in range(2):
            ps, pk = newps()
            for j in range(2):
                tb = tb2 * 2 + j
                mm(ps[:, j * 256:(j + 1) * 256], cst["urev"][:], sp_tm[:, tb, :], True, True,
                   ["c_urev", ("sp_tm", tb2)], [pk])
            act(Ebl[:, tb2 * 2:tb2 * 2 + 2, :], ps[:, :].rearrange("p (a b) -> p a b", a=2), AF.Exp, [pk],
                [("Ebl", tb2)])
        for h in range(4):
            ps, pk = newps()
            for tb in range(4):
                mm(ps[0:64, tb * 128:(tb + 1) * 128], sp_tm[:, tb, h * 64:(h + 1) * 64], cst["uinc"][:], True, True,
                   ["c_uinc", ("sp_tm", tb // 2)], [pk])
            act(Eb[:, h, :], ps[0:64, :], AF.Exp, [pk], [("Eb", h)])
            act(Enb[:, h, :], ps[0:64, :], AF.Exp, [pk], [("Enb", h)], scale=-1.0)
        wt, wk = wload("a_w_in", 0, 512)
        for h in range(4):
            ps, pk = newps()
            for k in range(8):
                mm(ps[0:64, :], wt[:, k, h * 64:(h + 1) * 64], hT[:, k, :], k == 0, k == 7, [wk, hk[k]], [pk])
            stt("dve", qtT[:, h, :], ps[0:64, :], 0.125, Eb[:, h, :], ALU.mult, ALU.mult, [pk, ("Eb", h)], [("qtT", h)])
            ps, pk = newps()
            for k in range(8):
                mm(ps[0:64, :], wt[:, k, 256 + h * 64:256 + (h + 1) * 64], hT[:, k, :], k == 0, k == 7, [wk, hk[k]], [pk])
            tt("dve", ktT[:, h, :], ps[0:64, :], Enb[:, h, :], ALU.mult, [pk, ("Enb", h)], [("ktT", h)])
        for tb in range(4):
            ps, pk = newps()
            for k in range(8):
                mm(ps[:, 0:256], hT[:, k, tb * 128:(tb + 1) * 128], wt[:, k, 256:512], k == 0, k == 7, [wk, hk[k]], [pk])
            tt("dve", kh_tm[:, tb, :], ps[:, 0:256], Ebl[:, tb, :], ALU.mult, [pk, ("Ebl", tb // 2)], [("kh_tm", tb)])
        wt, wk = wload("a_w_in", 512, 512)
        for tb in range(4):
            ps, pk = newps()
            for k in range(8):
                mm(ps[:, :], hT[:, k, tb * 128:(tb + 1) * 128], wt[:, k, :], k == 0, k == 7, [wk, hk[k]], [pk])
            cp("act", v_tm[:, tb, :], ps[:, :], [pk], [("v_tm", tb)])
        wt, wk = wload("a_w_in", 1024, 512)
        for h in range(4):
            ps, pk = newps()
            for k in range(8):
                mm(ps[:, :], wt[:, k, h * 128:(h + 1) * 128], hT[:, k, :], k == 0, k == 7, [wk, hk[k]], [pk])
            act(sr[:, h, :], ps[:, :], AF.Silu, [pk], [("sr", h)])
        wt, wk = wload("a_w_in", 1552, 512)
        for h in range(4):
            ps, pk = newps()
            for k in range(8):
                mm(ps[:, :], wt[:, k, h * 128:(h + 1) * 128], hT[:, k, :], k == 0, k == 7, [wk, hk[k]], [pk])
            act(mqT[:, h, :], ps[:, :], AF.Copy, [pk], [("mqT", h)], scale=float(128 ** -0.5))
        for tb in range(4):
            bs = slice(tb * 128, (tb + 1) * 128)
            ps, pk = newps()
            for h in range(4):
                mm(ps[:, h * 128:(h + 1) * 128], ktT[:, h, bs], qtT[:, h, bs], True, True, [("ktT", h), ("qtT", h)], [pk])
            tt("dve", A_sb[:, :], ps[:, :], cst["mask4"][:], ALU.mult, [pk, "c_mask4"], ["A_sb"])
            pso, pko = newps()
            for h in range(4):
                mm(pso[:, h * 128:(h + 1) * 128], v_tm[:, tb, h * 128:(h + 1) * 128], A_sb[:, h * 128:(h + 1) * 128],
                   True, False, [("v_tm", tb), "A_sb"], [pko])
                mm(pso[:, h * 128:(h + 1) * 128], Sbf[:, h, :], qtT[:, h, bs], False, True, ["Sbf", ("qtT", h)], [pko])
            cp("act", o_sb[:, :], pso[:, :], [pko], ["o_sb"])
            act(osq[:, :], pso[:, :], AF.Square, [pko], ["osq"])
            ps2, pk2 = newps()
            mm(ps2[:, :], cst["onesv"][:], osq[:, :], True, True, ["c_onesv", "osq"], [pk2])
            rsqrt_eps(orst[:, :], ps2[:, :], [pk2], "rstd")
            stt("dve", o_sb[:, :], o_sb[:, :], ghead[:, 0:1], orst[:, :], ALU.mult, ALU.mult, ["o_sb", "rstd", "ghead"],
                ["o_sb"])
            tt("dve", catT[:, 0:4, bs], o_sb[:, :].rearrange("p (h t) -> p h t", h=4), sr[:, :, bs], ALU.mult,
               ["o_sb"] + [("sr", h) for h in range(4)], [("catT", h) for h in range(4)])
            ps3, pk3 = newps()
            for h in range(4):
                mm(ps3[0:64, h * 128:(h + 1) * 128], kh_tm[:, tb, h * 64:(h + 1) * 64], v_tm[:, tb, h * 128:(h + 1) * 128],
                   True, True, [("kh_tm", tb), ("v_tm", tb)], [pk3])
            for h in range(4):
                stt("dve", S[:, h, :], S[:, h, :], Eb[:, h, tb * 128 + 127:tb * 128 + 128], ps3[0:64, h * 128:(h + 1) * 128],
                    ALU.mult, ALU.add, ["S", ("Eb", h), pk3], ["S"])
            cp("act", Sbf[:, :, :], S[:, :, :], ["S"], ["Sbf"])
        mem_attend(0)
        outproj("a_w_out")
        if dbg:
            for k in range(8):
                pass
        ffn(0, tix == 0)
        for k in range(8):
            p.dma("pool", xT_d[k * 128:(k + 1) * 128, t0:t0 + TT], xT[:, k, :], reads=[xk[k]], writes=[("xT_d", tix, k)])
        rms_fm(xT, xk, 6, TT, hT, "hT")
        wt, wk = wload("w_kv", 0, 512)
        wt2, wk2 = wload("w_kv", 512, 256)
        for tb in range(4):
            ps, pk = newps()
            for k in range(8):
                mm(ps[:, :], hT[:, k, tb * 128:(tb + 1) * 128], wt[:, k, :], k == 0, k == 7, [wk, hk[k]], [pk])
            cp("act", kvst[:, 0:512], ps[:, :], [pk], ["kvst"])
            ps, pk = newps()
            for k in range(8):
                mm(ps[:, 0:256], hT[:, k, tb * 128:(tb + 1) * 128], wt2[:, k, 0:256], k == 0, k == 7, [wk2, hk[k]], [pk])
            cp("dve", kvst[:, 512:768], ps[:, 0:256], [pk], ["kvst"])
            p.dma("pool", kvtm_d[t0 + tb * 128:t0 + (tb + 1) * 128, :], kvst[:, :], reads=["kvst"],
                  writes=[("kvtm_d", tix, tb)])
        for br in range(2):
            for g in range(2):
                c0 = (2 + 2 * br) * 128 + g * 64
                wsrc, wsk = (wt, wk) if c0 < 512 else (wt2, wk2)
                cc = c0 if c0 < 512 else c0 - 512
                ps, pk = newps()
                for k in range(8):
                    mm(ps[0:64, :], wsrc[:, k, cc:cc + 64], hT[:, k, :], k == 0, k == 7, [wsk, hk[k]], [pk])
                cp("act", kaT_sb[:, br * 2 + g, :], ps[0:64, :], [pk], [("kaT_sb", br * 2 + g)])
                p.dma("pool", kaT_d[br, g, :, t0:t0 + TT], kaT_sb[:, br * 2 + g, :], reads=[("kaT_sb", br * 2 + g)],
                      writes=[("kaT_d", br, g, tix)])

    p.barrier()
    p.aoff = markA
    flat = p.sb("flat", [128, 2048], F32)
    peb = p.sb("peb", [128, 2, 2048], F32)
    flatT = p.sb("flatT", [128, 16, 128], BF16)
    u_sb = p.sb("u_sb", [128, 2, 128], F32)
    t_sb = p.sb("t_sb", [128, 2, 128], F32)
    gT = p.sb("gT", [128, 2, 128], BF16)
    kc_sb = p.sb("kc_sb", [64, 128], BF16)
    vc_sb = p.sb("vc_sb", [128, 64], BF16)
    p.dma("sp", peb[:, :, :], pe_d[:, :, :], writes=["peb"])
    for which in range(2):
        w1t, w1k = wload("w_ck1" if which == 0 else "w_cv1", 0, 256, krows=2048)
        w2t, w2k = wload("w_ck2" if which == 0 else "w_cv2", 0, 64, krows=256)
        for g in range(2):
            c0 = which * 128 + g * 64
            for nt in range(NCT):
                nn = min(128, NCMP - nt * 128)
                src = bass.AP(tensor=kvtm_d.tensor, offset=(16 * 128 * nt) * 768 + c0,
                              ap=[[16 * 768, nn], [768, 32], [1, 64]])
                p.dma("sp", flat[0:nn, :].rearrange("p (a b) -> p a b", a=32), src, writes=["flat"])
                tt("dve", flat[0:nn, :], flat[0:nn, :], peb[0:nn, which, :], ALU.add, ["flat", "peb"], ["flat"])
                for half in range(4):
                    ps, pk = newps()
                    for cc in range(4):
                        c = half * 4 + cc
                        tr(ps[:, cc * 128:cc * 128 + nn], flat[0:nn, c * 128:(c + 1) * 128], ["flat"], [pk], n=nn)
                    cp("act" if half % 2 else "dve", flatT[:, half * 4:half * 4 + 4, 0:nn],
                       ps[:, :].rearrange("p (a b) -> p a b", a=4)[:, :, 0:nn], [pk], [("flatT", half)])
                for hc in range(2):
                    ps, pk = newps()
                    for c in range(16):
                        mm(ps[:, 0:nn], w1t[:, c, hc * 128:(hc + 1) * 128], flatT[:, c, 0:nn], c == 0, c == 15,
                           [w1k, ("flatT", c // 4)], [pk])
                    uu, tq = u_sb[:, hc, 0:nn], t_sb[:, hc, 0:nn]
                    cp("act", uu, ps[:, 0:nn], [pk], [("u", hc)])
                    act(tq, ps[:, 0:nn], AF.Square, [pk], [("t", hc)])
                    ts("dve", tq, tq, 0.044715, 1.0, ALU.mult, ALU.add, [("t", hc)], [("t", hc)])
                    tt("dve", tq, tq, uu, ALU.mult, [("t", hc), ("u", hc)], [("t", hc)])
                    act(tq, tq, AF.Tanh, [("t", hc)], [("t", hc)], scale=0.7978845608028654)
                    stt("dve", tq, tq, 1.0, uu, ALU.add, ALU.mult, [("t", hc), ("u", hc)], [("t", hc)])
                    act(gT[:, hc, 0:nn], tq, AF.Copy, [("t", hc)], [("gT", hc)], scale=0.5)
                ps, pk = newps()
                if which == 0:
                    for hc in range(2):
                        mm(ps[0:64, 0:nn], w2t[:, hc, 0:64], gT[:, hc, 0:nn], hc == 0, hc == 1, [w2k, ("gT", hc)], [pk])
                    cp("act", kc_sb[:, 0:nn], ps[0:64, 0:nn], [pk], ["kc_sb"])
                    p.dma("pool", kcT_d[g, :, nt * 128:nt * 128 + nn], kc_sb[:, 0:nn], reads=["kc_sb"],
                          writes=[("kcT_d", g, nt)])
                else:
                    for hc in range(2):
                        mm(ps[0:nn, 0:64], gT[:, hc, 0:nn], w2t[:, hc, 0:64], hc == 0, hc == 1, [w2k, ("gT", hc)], [pk])
                    cp("act", vc_sb[0:nn, :], ps[0:nn, 0:64], [pk], ["vc_sb"])
                    p.dma("pool", vc_d[g, nt * 128:nt * 128 + nn, :], vc_sb[0:nn, :], reads=["vc_sb"],
                          writes=[("vc_d", g, nt)])

    p.barrier()
    p.dma("sp", wgl[:, :, :], WB["wgp"][:, :].rearrange("(k p) f -> p k f", p=128), writes=["wgl"])
    for tix in range(NT):
        t0 = tix * TT
        for k in range(8):
            p.dma("sp", xT[:, k, :], xT_d[k * 128:(k + 1) * 128, t0:t0 + TT], writes=[xk[k]])
        rms_fm(xT, xk, 1, TT, hT, "hT")
        wt, wk = wload("b_w_in", 0, 512)
        for h in range(8):
            ps, pk = newps()
            for k in range(8):
                mm(ps[0:64, :], wt[:, k, h * 64:(h + 1) * 64], hT[:, k, :], k == 0, k == 7, [wk, hk[k]], [pk])
            dst = (qtT if h < 4 else ktT)[:, h % 4, :]
            act(dst, ps[0:64, :], AF.Copy, [pk], [("qh", h)], scale=0.125)
            p.dma("pool", qT_d[h, :, t0:t0 + TT], dst, reads=[("qh", h)], writes=[("qT_d", h, tix)])
        ps, pk = newps()
        for k in range(8):
            mm(ps[0:24, :], wgl[:, k, :], hT[:, k, :], k == 0, k == 7, ["wgl", hk[k]], [pk])
        act(gat_sb[0:24, :], ps[0:24, :], AF.Sigmoid, [pk], ["gat_sb"])
        p.dma("pool", gatesT_d[:, t0:t0 + TT], gat_sb[0:24, :], reads=["gat_sb"], writes=[("gatesT_d", tix)])
        wt, wk = wload("b_w_in", 536, 512)
        for h in range(4):
            ps, pk = newps()
            for k in range(8):
                mm(ps[:, :], wt[:, k, h * 128:(h + 1) * 128], hT[:, k, :], k == 0, k == 7, [wk, hk[k]], [pk])
            act(mqT[:, h, :], ps[:, :], AF.Copy, [pk], [("mqT", h)], scale=float(128 ** -0.5))
        mem_attend(1)
        for h in range(4):
            p.dma("pool", mT_d[h * 128:(h + 1) * 128, t0:t0 + TT], catT[:, 4 + h, :], reads=[("catT", 4 + h)],
                  writes=[("mT_d", h, tix)])

    p.barrier()
    p.aoff = markA
    NC = {k: p.sb("n_" + k, list(v.shape), BF16 if v.dtype != np.float32 else F32) for k, v in ncst.items()
          if k not in ("augk", "augkc", "augq")}
    KsT = p.sb("KsT", [68, T], BF16)
    KwT = p.sb("KwT", [68, T], BF16)
    KcT = p.sb("KcT", [68, NCP], BF16)
    VsA = p.sb("VsA", [128, NKB, 65], BF16)
    VwA = p.sb("VwA", [128, NKB, 65], BF16)
    VcA = p.sb("VcA", [128, NCT, 65], BF16)
    Qaug = [p.sb("Qaug%d" % i, [68, 512], BF16) for i in range(2)]
    gt = [p.sb("gt%d" % i, [65, 3, 512], F32) for i in range(2)]
    PcT = p.sb("PcT", [128, NCT, 512], BF16)
    PsT = [p.sb("PsT%d" % i, [128, 512], BF16) for i in range(4)]
    impt = p.sb("impt", [128, 128], F32)
    impw = p.sb("impw", [128, 128], F32)
    m8 = p.sb("m8", [128, 16], F32)
    selneg = p.sb("selneg", [128, 128], F32)
    selT4 = [p.sb("selT4_%d" % i, [128, 512], BF16) for i in range(2)]
    rz4 = p.sb("rz4", [128, 4], F32)
    oT_sb = [p.sb("oT_sb%d" % i, [65, 512], F32) for i in range(3)]
    Rrow = [p.sb("Rrow%d" % i, [65, 512], F32) for i in range(3)]
    otmp = p.sb("otmp", [64, 512], F32)
    ocT = [p.sb("ocT%d" % i, [64, 512], F32) for i in range(2)]
    ocTb = [p.sb("ocTb%d" % i, [64, 512], BF16) for i in range(2)]
    print("arena words P4", p.aoff)
    for k in NC:
        v = ncst_d[k]
        p.dma("sp", NC[k], v, writes=["n_" + k])
    st["psmod"] = 4
    st["ps"] = 0
    BIG = 1.0e9

    deferred = []

    def defer(n, fn):
        deferred.append([n, fn])

    def tick():
        for d_ in deferred:
            d_[0] -= 1
        while deferred and deferred[0][0] <= 0:
            deferred.pop(0)[1]()

    def flush():
        while deferred:
            deferred.pop(0)[1]()

    def fin1(pv, pvk, br, qi):
        o_, r_ = oT_sb[br], Rrow[br]
        cp("act", o_[:, :], pv[0:65, :], [pvk], [("oT_sb", br)])
        ts("dve", r_[64:65, :], o_[64:65, :], 1e-30, None, ALU.max, None, [("oT_sb", br)], [("Rrow", br)])
        act(r_[64:65, :], r_[64:65, :], AF.Ln, [("Rrow", br)], [("Rrow", br)])
        act(r_[64:65, :], r_[64:65, :], AF.Exp, [("Rrow", br)], [("Rrow", br)], scale=-1.0)
        tt("dve", r_[64:65, :], r_[64:65, :], gt[qi][64:65, br, :], ALU.mult, [("Rrow", br), ("gt", qi)], [("Rrow", br)])

    def fin2(br, qi, first):
        o_, r_ = oT_sb[br], Rrow[br]
        prb, prbk = newps()
        mm(prb[0:64, :], NC["ones1"][64:65, 0:64], r_[64:65, :], True, True, ["n_ones1", ("Rrow", br)], [prbk])
        if first:
            tt("dve", ocT[qi][:, :], o_[0:64, :], prb[0:64, :], ALU.mult, [("oT_sb", br), prbk], [("ocT", qi)])
        else:
            tt("dve", otmp[:, :], o_[0:64, :], prb[0:64, :], ALU.mult, [("oT_sb", br), prbk], ["otmp"])
            tt("pool", ocT[qi][:, :], ocT[qi][:, :], otmp[:, :], ALU.add, ["otmp", ("ocT", qi)], [("ocT", qi)])

    def unit_s(u):
        ps, pk = newps()
        kb = u["kb"]
        extra = u["extra"]
        mm(ps[:, :], u["K"][:, kb * 128:(kb + 1) * 128], u["Q"][:, :], True, len(extra) == 0, [u["kkey"], u["qkey"]], [pk])
        for ei, (l_ap, r_ap, rk) in enumerate(extra):
            mm(ps[:, :], l_ap, r_ap, False, ei == len(extra) - 1, rk, [pk])
        pi = st["pt"]
        st["pt"] = (pi + 1) % len(PsT)
        act(PsT[pi][:, :], ps[:, :], AF.Exp, [pk], [("PsT", pi)])
        return pi

    def unit_pv(u, pi):
        mm(u["pv"][0:65, :], u["V"][:, u["kb"], :], PsT[pi][:, :], u["first"], u["last"], [("PsT", pi), u["vkey"]],
           [u["pvk"]])
        if u["after"] is not None:
            u["after"]()
        tick()

    def stage_A(g, qb):
        t0 = qb * 128
        qi = qb % 2
        Q, qkey = Qaug[qi], ("Q", qi)
        p.dma("sp", Q[0:64, :].rearrange("d (h q) -> d h q", h=4),
              qT_d[4 * g:4 * g + 4, :, t0:t0 + 128].rearrange("h d q -> d h q"), writes=[qkey])
        p.dma("sp", Q[64:68, :], ncst_d["augq"][g, qb, :, :], writes=[qkey])
        p.dma("sp", gt[qi][64:65, :, :].rearrange("o b (j q) -> o (b j) q", j=4),
              gatesT_d[12 * g:12 * g + 12, t0:t0 + 128].rearrange("(o r) q -> o r q", o=1), writes=[("gt", qi)])
        ncc = qb // 16 + 1
        for c in range(ncc):
            m = qb - 16 * c
            ps, pk = newps()
            mm(ps[:, :], KcT[:, c * 128:(c + 1) * 128], Q[:, :], True, m > 16, ["KcT", qkey], [pk])
            if m <= 16:
                mm(ps[:, :], NC["identb"][:, :], NC["mc"][:, m, :], False, True, ["n_identb", "n_mc"], [pk])
            act(PcT[:, c, :], ps[:, :], AF.Exp, [pk], [("PcT", c)])
        pvc, pvck = psum[4], ("ps", 4)
        for c in range(ncc):
            mm(pvc[0:65, :], VcA[:, c, :], PcT[:, c, :], c == 0, c == ncc - 1, [("PcT", c), "VcA"], [pvck])
        for j in range(4):
            pim, pimk = psum[6 + j // 2], ("ps", 6 + j // 2)
            o0 = (j % 2) * 129
            for c in range(ncc):
                mm(pim[:, o0:o0 + 129], PcT[:, c, j * 128:(j + 1) * 128], NC["ova"][:, c, :], c == 0,
                   c == ncc - 1, [("PcT", c), "n_ova"], [pimk])
        fin1(pvc, pvck, 0, qi)
        defer(3, lambda: fin2(0, qi, True))
        for hb in range(2):
            zv = psum[6 + hb][:, 0:258].rearrange("p (j e) -> p j e", e=129)[:, :, 128]
            ts("dve", rz4[:, 2 * hb:2 * hb + 2], zv, 1e-30, None, ALU.max, None, [("ps", 6 + hb)], ["rz4"])
        p.op("dve", lambda e: e.reciprocal(out=rz4[:, :], in_=rz4[:, :]), ["rz4"], ["rz4"])
        ts("dve", impt[:, :], psum[6][:, 0:128], rz4[:, 0:1], None, ALU.mult, None, [("ps", 6), "rz4"], ["impt"])
        for j in range(1, 4):
            o0 = (j % 2) * 129
            stt("dve", impt[:, :], psum[6 + j // 2][:, o0:o0 + 128], rz4[:, j:j + 1], impt[:, :], ALU.mult, ALU.add,
                [("ps", 6 + j // 2), "rz4", "impt"], ["impt"])
        f0 = 126 - 2 * qb
        tt("dve", impt[:, :], impt[:, :], NC["ftab"][:, f0:f0 + 128], ALU.add, ["impt", "n_ftab"], ["impt"])
        memset("dve", impt[:, 0:1], BIG, ["impt"])
        p.op("dve", lambda e: e.max(out=m8[:, 0:8], in_=impt[:, :]), ["impt"], ["m8"])
        p.op("dve", lambda e: e.match_replace(out=impw[:, :], in_to_replace=m8[:, 0:8], in_values=impt[:, :],
                                              imm_value=-3.0e38), ["impt", "m8"], ["impw"])
        p.op("dve", lambda e: e.max(out=m8[:, 8:16], in_=impw[:, :]), ["impw"], ["m8"])
        ts("dve", selneg[:, :], impt[:, :], m8[:, 15:16], None, ALU.is_lt, None, ["impt", "m8"], ["selneg"])
        ts("dve", selneg[:, :], selneg[:, :], -30000.0, None, ALU.mult, None, ["selneg"], ["selneg"])

        def sel_transpose():
            pst, pstk = newps()
            tr(pst[:, 0:128], selneg[:, :], ["selneg"], [pstk])
            for j in range(4):
                cp("act" if j % 2 else "dve", selT4[qi][:, j * 128:(j + 1) * 128], pst[:, 0:128], [pstk],
                   [("selT4", qi)])
        defer(9, sel_transpose)

    def stage_B(g, qb, nxt):
        t0 = qb * 128
        qi = qb % 2
        Q, qkey = Qaug[qi], ("Q", qi)
        pvs, pvsk = psum[5], ("ps", 5)
        pvw, pvwk = psum[4], ("ps", 4)
        units = []

        def slc_done():
            fin1(pvs, pvsk, 1, qi)
            defer(2, lambda: fin2(1, qi, False))

        for kb in range(qb + 1):
            extra = [(NC["expm"][:, kb, :], selT4[qi][:, :], ["n_expm", ("selT4", qi)])]
            if kb == qb:
                extra.append((NC["identb"][:, :], NC["mdiag"][:, :], ["n_identb", "n_mdiag"]))
            units.append(dict(K=KsT, kkey="KsT", V=VsA, vkey="VsA", kb=kb, Q=Q, qkey=qkey, extra=extra, pv=pvs, pvk=pvsk,
                              first=kb == 0, last=kb == qb, after=slc_done if kb == qb else None))
        kbs = [kb for kb in range(qb - 4, qb + 1) if kb >= 0]

        def win_done():
            fin1(pvw, pvwk, 2, qi)

            def fin_store():
                fin2(2, qi, False)
                cp("act", ocTb[qi][:, :], ocT[qi][:, :], [("ocT", qi)], [("ocTb", qi)])
                p.dma("pool", oT_d[256 * g:256 * (g + 1), t0:t0 + 128].rearrange("(j d) q -> d j q", j=4),
                      ocTb[qi][:, :].rearrange("d (j q) -> d j q", j=4), reads=[("ocTb", qi)],
                      writes=[("oT_d", g, qb)])
            defer(3, fin_store)

        for kb in kbs:
            extra = []
            if kb == qb - 4:
                extra.append((NC["identb"][:, :], NC["mfar"][:, :], ["n_identb", "n_mfar"]))
            if kb == qb:
                extra.append((NC["identb"][:, :], NC["mdiag"][:, :], ["n_identb", "n_mdiag"]))
            units.append(dict(K=KwT, kkey="KwT", V=VwA, vkey="VwA", kb=kb, Q=Q, qkey=qkey, extra=extra, pv=pvw, pvk=pvwk,
                              first=kb == kbs[0], last=kb == qb, after=win_done if kb == qb else None))
        pend = []
        for ui, u in enumerate(units):
            pi = unit_s(u)
            pend.append((u, pi))
            if len(pend) > 2:
                unit_pv(*pend.pop(0))
            if ui == 1 and nxt is not None:
                stage_A(g, nxt)
        if len(units) < 2 and nxt is not None:
            stage_A(g, nxt)
        while pend:
            unit_pv(*pend.pop(0))
        keep = [d_ for d_ in deferred if getattr(d_[1], "__name__", "") == "fin_store"]
        rest = [d_ for d_ in deferred if d_ not in keep]
        deferred[:] = rest
        flush()
        deferred[:] = keep

    for g in range(2):
        p.dma("sp", KsT[0:64, :], kaT_d[0, g, :, :], writes=["KsT"])
        p.dma("sp", KsT[64:68, :], ncst_d["augk"][:, :], writes=["KsT"])
        p.dma("sp", KwT[0:64, :], kaT_d[1, g, :, :], writes=["KwT"])
        p.dma("sp", KwT[64:68, :], ncst_d["augk"][:, :], writes=["KwT"])
        memset("dve", KcT[:, :], 0.0, ["KcT"])
        p.dma("sp", KcT[0:64, 0:NCMP], kcT_d[g, :, 0:NCMP], writes=["KcT"])
        p.dma("sp", KcT[64:68, :], ncst_d["augkc"][:, :], writes=["KcT"])
        memset("dve", VcA[:, :, :], 0.0, ["VcA"])
        for c in range(NCT):
            nn = min(128, NCMP - c * 128)
            p.dma("sp", VcA[0:nn, c, 0:64], vc_d[g, c * 128:c * 128 + nn, :], writes=["VcA"])
        memset("dve", VcA[:, :, 64:65], 1.0, ["VcA"])
        for (Va, vkey, ci) in ((VsA, "VsA", 3), (VwA, "VwA", 5)):
            cc0 = ci * 128 + g * 64
            p.dma("pool", Va[:, :, 0:64], kvtm_d[:, cc0:cc0 + 64].rearrange("(kb p) d -> p kb d", p=128),
                  writes=[vkey])
            memset("dve", Va[:, :, 64:65], 1.0, [vkey])
        stage_A(g, 0)
        flush()
        for qb in range(NQB):
            stage_B(g, qb, qb + 1 if qb + 1 < NQB else None)
        flush()
    st["psmod"] = 8
    st["ps"] = 0

    p.barrier()
    fin = []
    for tix in range(NT):
        t0 = tix * TT
        for k in range(8):
            p.dma("sp", xT[:, k, :], xT_d[k * 128:(k + 1) * 128, t0:t0 + TT], writes=[xk[k]])
        for h in range(4):
            p.dma("sp", catT[:, h, :], oT_d[h * 128:(h + 1) * 128, t0:t0 + TT], writes=[ck[h]])
        for h in range(4):
            p.dma("sp", catT[:, 4 + h, :], mT_d[h * 128:(h + 1) * 128, t0:t0 + TT], writes=[ck[4 + h]])
        outproj("b_w_out")
        ffn(1, tix == 0)
        ps, pk = newps()
        for k in range(8):
            j = st["sq"]
            st["sq"] = 1 - j
            act(sqb[j][:, :], xT[:, k, :], AF.Square, [xk[k]], [("sq", j)])
            mm(ps[:, :], cst["onesd"][:], sqb[j][:, :], k == 0, k == 7, [("sq", j), "c_onesd"], [pk])
        rsqrt_eps(rstd[:, :], ps[:, :], [pk], "rstd")
        for k in range(8):
            stt("dve", xT[:, k, :], xT[:, k, :], gv[:, 7, k:k + 1], rstd[:, :], ALU.mult, ALU.mult,
                [xk[k], "rstd", "gv"], [xk[k]])
        for tb in range(4):
            for half in range(2):
                ps, pk = newps()
                for kk in range(4):
                    k = half * 4 + kk
                    tr(ps[:, kk * 128:(kk + 1) * 128], xT[:, k, tb * 128:(tb + 1) * 128], [xk[k]], [pk])
                cp("act" if half else "dve", xin[:, tb % 2, half * 512:(half + 1) * 512], ps[:, :], [pk],
                   [("xin", tb % 2)])
            p.dma("pool", out_d[t0 + tb * 128:t0 + (tb + 1) * 128, :], xin[:, tb % 2, :], reads=[("xin", tb % 2)],
                  writes=[("out", tix, tb)])
            fin.append(("out", tix, tb))
    p.barrier()
    p.op("sp", lambda e: e.nop(), reads=fin)
    nc = p.finalize()
    return nc, p


def nsa_consts(T):
    bf = ml_dtypes.bfloat16
    NCMP = T // 16 - 1
    NCT = (NCMP + 127) // 128
    NCP = NCT * 128
    NKB = T // 128
    NQB = T // 128
    NEG = -30000.0
    c = {}
    c["identb"] = np.eye(128, dtype=np.float32).astype(bf)
    pos = np.arange(T)
    c["augk"] = np.stack([np.ones(T), np.ones(T), pos % 128, pos - pos % 128]).astype(np.float32).astype(bf)
    e = 16 * np.arange(NCP) + 31
    c["augkc"] = np.stack([np.ones(NCP), np.ones(NCP), e % 128, e - e % 128]).astype(np.float32).astype(bf)
    aq = np.zeros((2, NQB, 4, 512), np.float32)
    i = np.arange(128)
    for g in range(2):
        for j in range(4):
            slope = 2.0 ** (-(4 * g + j + 1))
            for qb in range(NQB):
                aq[g, qb, 0, j * 128:(j + 1) * 128] = -slope * i
                aq[g, qb, 1, j * 128:(j + 1) * 128] = -slope * (128 * qb)
                aq[g, qb, 2, j * 128:(j + 1) * 128] = slope
                aq[g, qb, 3, j * 128:(j + 1) * 128] = slope
    c["augq"] = aq.astype(bf)
    kj = np.arange(128)[:, None]
    qi = np.arange(128)[None, :]
    c["mdiag"] = np.tile(np.where(kj > qi, NEG, 0.0), (1, 4)).astype(np.float32).astype(bf)
    c["mfar"] = np.tile(np.where(kj <= qi, NEG, 0.0), (1, 4)).astype(np.float32).astype(bf)
    mc = np.zeros((128, 17, 512), np.float32)
    for m in range(17):
        mc[:, m, :] = np.tile(np.where(16 * kj + 31 - qi > 128 * m, NEG, 0.0), (1, 4))
    c["mc"] = mc.astype(bf)
    n = np.arange(NCP)[:, None]
    jb = np.arange(T // 64)[None, :]
    ov = np.clip(np.minimum(16 * n + 32, 64 * jb + 64) - np.maximum(16 * n, 64 * jb), 0, None) / 32.0
    ov[NCMP:] = 0.0
    ovp = np.zeros((NCP, 128), np.float32)
    ovp[:, :min(128, T // 64)] = ov[:, :128]
    ova = np.zeros((NCP, 129), np.float32)
    ova[:, :128] = ovp
    ova[:NCMP, 128] = 1.0
    c["ova"] = ova.reshape(NCT, 128, 129).transpose(1, 0, 2).astype(bf)
    c["ones1"] = np.ones((65, 64), np.float32)
    ex = np.zeros((128, NKB, 128), np.float32)
    for kb in range(NKB):
        for r in range(2):
            if 2 * kb + r < 128:
                ex[2 * kb + r, kb, r * 64:(r + 1) * 64] = 1.0
    c["expm"] = ex.astype(bf)
    ft = np.zeros((128, 256), np.float32)
    hi = (np.arange(128) >= 64).astype(np.int64)[:, None]
    r = np.arange(256)[None, :] - 126
    ft = np.where((r == hi) | (r == hi - 1), 1.0e9, np.where(r > hi, -1.0e9, 0.0)).astype(np.float32)
    c["ftab"] = ft
    return c


def host_inputs(inputs, b, T):
    f = lambda a: np.ascontiguousarray(a, dtype=np.float32)
    m = {}
    m["x"] = f(inputs["x"][b, :T])
    m["mem"] = f(inputs["mem"][b])
    gl = [inputs["g_mix"][0], inputs["g_mix"][1], inputs["g_ffn"][0], inputs["g_ffn"][1], inputs["g_mem"][0],
          inputs["g_mem"][1], inputs["g_kv"], inputs["g_final"]]
    m["gvec"] = f(np.stack([np.asarray(g).reshape(8, 128).T for g in gl], axis=1))
    cw = np.asarray(inputs["conv_w"])
    m["convw"] = f(cw.reshape(2, 3, NFC, 128).transpose(3, 0, 2, 1))
    m["convb"] = f(np.asarray(inputs["conv_b"]).reshape(2, NFC, 128).transpose(2, 0, 1))
    m["ghead"] = f(np.asarray(inputs["a_g_head"]).reshape(128, 1))
    wa = np.zeros((32, 256), np.float32)
    wa[0:16] = np.asarray(inputs["a_w_alpha"])[0]
    wa[16] = np.asarray(inputs["a_b_alpha"])[0]
    m["walpha"] = wa
    m["a_w_in"] = f(inputs["a_w_in"][0]); m["a_w_out"] = f(inputs["a_w_out"][0])
    perm = [(4 * g + j) * 3 + br for g in range(2) for br in range(3) for j in range(4)]
    m["wgp"] = f(np.asarray(inputs["b_w_in"][0])[:, 512:536][:, perm])
    m["w_mem_kv0"] = f(inputs["w_mem_kv"][0]); m["w_mem_kv1"] = f(inputs["w_mem_kv"][1])
    m["w_up0"] = f(inputs["w_up"][0]); m["w_up1"] = f(inputs["w_up"][1])
    m["w_down0"] = f(inputs["w_down"][0]); m["w_down1"] = f(inputs["w_down"][1])
    m["w_kv"] = f(inputs["w_kv"]); m["b_w_in"] = f(inputs["b_w_in"][0]); m["b_w_out"] = f(inputs["b_w_out"][0])
    for k in ("w_ck1", "w_ck2", "w_cv1", "w_cv2"):
        m[k] = f(inputs[k])
    pe = np.stack([np.asarray(inputs["pe_k"]).reshape(2048), np.asarray(inputs["pe_v"]).reshape(2048)], axis=0)
    m["pe_kv"] = f(np.broadcast_to(pe[None], (128, 2, 2048)))
    for k, v in host_consts().items():
        m["c_" + k] = v
    for k, v in nsa_consts(T).items():
        m["n_" + k] = v
    return m


def kernel(**inputs):
    T = inputs["x"].shape[1]
    nc, _ = build(T)
    in_maps = [host_inputs(inputs, b % 4, T) for b in range(8)]
    res = run_bass_kernel_spmd(nc, in_maps, core_ids=list(range(8)))
    return np.stack([res.results[b]["out"] for b in range(4)], axis=0).astype(np.float32)
```
